# Optimizing a Trainium2 kernel written in Bass

```python
import math
import jax, jax.numpy as jnp
from jax import lax
import numpy as np

D_MODEL = 1024
BATCH = 4
SEQ = 4096
DEPTH = 4
DEC_BATCH = 1
DEC_SEQ = 16384
PAST_LEN = 128

PLE_DIM = 256
N_MIXERS = 2
N_SSD_LAYERS = (DEPTH + 1) // 2
N_ATT_LAYERS = DEPTH // 2
EPS = 1e-6

SSD_EXPAND = 2
SSD_INNER = SSD_EXPAND * D_MODEL
SSD_HEAD_DIM = 64
SSD_HEADS = SSD_INNER // SSD_HEAD_DIM
SSD_GROUPS = 8
SSD_HPG = SSD_HEADS // SSD_GROUPS
SSD_STATE = 128
SSD_GN = SSD_GROUPS * SSD_STATE
SSD_CONV_W = 5
SSD_CHUNK = 128
SSD_CONV_CH = SSD_INNER + 2 * SSD_GN
SSD_IN_COLS = SSD_INNER + SSD_CONV_CH + 2 * SSD_HEADS

ATT_WIDTH = D_MODEL
ATT_HEADS = 8
ATT_QK_DIM = ATT_WIDTH // (2 * ATT_HEADS)
ATT_V_DIM = 2 * ATT_QK_DIM
ATT_HD = ATT_HEADS * ATT_QK_DIM
ATT_Q_BLOCK = 128
ATT_IN_COLS = 4 * ATT_HD + ATT_HEADS * ATT_V_DIM + ATT_WIDTH

kernel_name = "hybrid_ssd_diffattn_bidir_encoder"


def rms_norm(x, g):
    xf = x.astype(jnp.float32)
    y = xf * lax.rsqrt(jnp.mean(xf * xf, axis=-1, keepdims=True) + EPS)
    return (y * g.astype(jnp.float32)).astype(x.dtype)


def centred_dwconv(x, w, b):
    pad = SSD_CONV_W // 2
    y = lax.conv_general_dilated(x, w[:, None, :].astype(x.dtype), window_strides=(1,),
                                 padding=[(pad, pad)],
                                 dimension_numbers=('NWC', 'WIO', 'NWC'),
                                 feature_group_count=x.shape[-1])
    return y + b.astype(x.dtype)


def ssd_chunked(x, dt, a, bm, cm):
    b, l = x.shape[0], x.shape[1]
    q = SSD_CHUNK
    nc = l // q
    xc = x.reshape(b, nc, q, SSD_GROUPS, SSD_HPG, SSD_HEAD_DIM)
    dtc = dt.reshape(b, nc, q, SSD_GROUPS, SSD_HPG)
    bc = bm.reshape(b, nc, q, SSD_GROUPS, SSD_STATE)
    cc = cm.reshape(b, nc, q, SSD_GROUPS, SSD_STATE)
    a_cs = jnp.cumsum(jnp.moveaxis(dtc * a, 2, -1), axis=-1)
    xdt = xc.astype(jnp.float32) * dtc[..., None]
    seg = a_cs[..., :, None] - a_cs[..., None, :]
    lower = jnp.tril(jnp.ones((q, q), dtype=bool))
    lmat = jnp.exp(jnp.where(lower, seg, -jnp.inf))
    cb = jnp.einsum('bclgn,bcsgn->bcgls', cc, bc).astype(jnp.float32)
    y_diag = jnp.einsum('bcgls,bcgrls,bcsgrp->bclgrp', cb, lmat, xdt)
    decay_states = jnp.exp(a_cs[..., -1:] - a_cs)
    states = jnp.einsum('bcsgn,bcgrs,bcsgrp->bcgrpn', bc.astype(jnp.float32), decay_states, xdt)
    chunk_decay = jnp.exp(a_cs[..., -1])

    def step(carry, inp):
        st, dec = inp
        return carry * dec[..., None, None] + st, carry

    init = jnp.zeros_like(states[:, 0])
    _, prev = lax.scan(step, init, (jnp.moveaxis(states, 1, 0), jnp.moveaxis(chunk_decay, 1, 0)))
    prev = jnp.moveaxis(prev, 0, 1)
    y_off = jnp.einsum('bclgn,bcgrpn,bcgrl->bclgrp', cc.astype(jnp.float32), prev, jnp.exp(a_cs))
    return (y_diag + y_off).reshape(b, l, SSD_GROUPS, SSD_HPG, SSD_HEAD_DIM)


def ssd_mixer(h, w_in, conv_w, conv_b, dt_bias, a_log, d_skip, norm_g, w_out):
    b, l, _ = h.shape
    proj = h @ w_in
    z = proj[..., :SSD_INNER]
    xbc = proj[..., SSD_INNER:SSD_INNER + SSD_CONV_CH]
    dt_raw = proj[..., SSD_INNER + SSD_CONV_CH:]
    xbc = jax.nn.silu(centred_dwconv(xbc, conv_w, conv_b))
    xs = xbc[..., :SSD_INNER].reshape(b, l, SSD_GROUPS, SSD_HPG, SSD_HEAD_DIM)
    bm = xbc[..., SSD_INNER:SSD_INNER + SSD_GN].reshape(b, l, SSD_GROUPS, SSD_STATE)
    cm = xbc[..., SSD_INNER + SSD_GN:].reshape(b, l, SSD_GROUPS, SSD_STATE)
    dt = jax.nn.softplus(dt_raw.astype(jnp.float32).reshape(b, l, 2, SSD_GROUPS, SSD_HPG)
                         + dt_bias.astype(jnp.float32).reshape(2, SSD_GROUPS, SSD_HPG))
    a = -jnp.exp(a_log.astype(jnp.float32)).reshape(2, SSD_GROUPS, SSD_HPG)
    y_f = ssd_chunked(xs, dt[:, :, 0], a[0], bm, cm)
    fl = lambda t: jnp.flip(t, axis=1)
    y_b = fl(ssd_chunked(fl(xs), fl(dt[:, :, 1]), a[1], fl(bm), fl(cm)))
    y = y_f + y_b + xs.astype(jnp.float32) * d_skip.astype(jnp.float32).reshape(SSD_GROUPS, SSD_HPG)[..., None]
    y = y.reshape(b, l, SSD_INNER).astype(h.dtype)
    y = rms_norm(y * jax.nn.silu(z), norm_g)
    return y @ w_out


def alibi_slopes():
    return jnp.asarray(np.array([2.0 ** (-8.0 * (i + 1) / ATT_HEADS) for i in range(ATT_HEADS)],
                                dtype=np.float32))


def diff_attention_mixer(h, w_in, q_norm_g, k_norm_g, lam_q1, lam_k1, lam_q2, lam_k2,
                         sub_norm_g, w_out, lambda_init):
    b, l, _ = h.shape
    proj = h @ w_in
    q = proj[..., :2 * ATT_HD].reshape(b, l, ATT_HEADS, 2, ATT_QK_DIM)
    k = proj[..., 2 * ATT_HD:4 * ATT_HD].reshape(b, l, ATT_HEADS, 2, ATT_QK_DIM)
    v = proj[..., 4 * ATT_HD:4 * ATT_HD + ATT_HEADS * ATT_V_DIM].reshape(b, l, ATT_HEADS, ATT_V_DIM)
    gate = proj[..., 4 * ATT_HD + ATT_HEADS * ATT_V_DIM:]
    q = rms_norm(q, q_norm_g) * (ATT_QK_DIM ** -0.5)
    k = rms_norm(k, k_norm_g)
    f32 = jnp.float32
    lam = (jnp.exp(jnp.sum(lam_q1.astype(f32) * lam_k1.astype(f32)))
           - jnp.exp(jnp.sum(lam_q2.astype(f32) * lam_k2.astype(f32))) + lambda_init)
    slopes = alibi_slopes()
    nq = l // ATT_Q_BLOCK
    qb = jnp.moveaxis(q.reshape(b, nq, ATT_Q_BLOCK, ATT_HEADS, 2, ATT_QK_DIM), 1, 0)
    pos_k = jnp.arange(l)

    def block(args):
        qi, start = args
        s = jnp.einsum('bqhmd,bkhmd->bhmqk', qi, k).astype(f32)
        pos_q = start + jnp.arange(ATT_Q_BLOCK)
        dist = jnp.abs(pos_q[:, None] - pos_k[None, :]).astype(f32)
        s = s - slopes[None, :, None, None, None] * dist
        pr = jax.nn.softmax(s, axis=-1)
        attn = pr[:, :, 0] - lam * pr[:, :, 1]
        return jnp.einsum('bhqk,bkhe->bqhe', attn.astype(v.dtype), v)

    starts = jnp.arange(nq, dtype=jnp.int32) * ATT_Q_BLOCK
    o = lax.map(block, (qb, starts))
    o = jnp.moveaxis(o, 0, 1).reshape(b, l, ATT_HEADS, ATT_V_DIM)
    o = rms_norm(o, sub_norm_g) * (1.0 - lambda_init)
    o = o.reshape(b, l, ATT_WIDTH) * jax.nn.silu(gate)
    return o @ w_out


def trunk(x, p, pre_norm_g, ssd_w_in, ssd_conv_w, ssd_conv_b, ssd_dt_bias, ssd_a_log,
          ssd_d_skip, ssd_norm_g, ssd_w_out, att_w_in, att_q_norm_g, att_k_norm_g,
          att_lam_q1, att_lam_k1, att_lam_q2, att_lam_k2, att_sub_norm_g, att_w_out,
          ple_w_proj, ple_norm_g, ple_gate_norm_g, ple_w_gate):
    for i in range(DEPTH):
        h = rms_norm(x, pre_norm_g[i])
        j = i // N_MIXERS
        if i % N_MIXERS == 0:
            mix = ssd_mixer(h, ssd_w_in[j], ssd_conv_w[j], ssd_conv_b[j], ssd_dt_bias[j],
                            ssd_a_log[j], ssd_d_skip[j], ssd_norm_g[j], ssd_w_out[j])
        else:
            lambda_init = 0.8 - 0.6 * math.exp(-0.3 * i)
            mix = diff_attention_mixer(h, att_w_in[j], att_q_norm_g[j], att_k_norm_g[j],
                                       att_lam_q1[j], att_lam_k1[j], att_lam_q2[j], att_lam_k2[j],
                                       att_sub_norm_g[j], att_w_out[j], lambda_init)
        x = x + mix.astype(x.dtype)
        g = jax.nn.sigmoid(rms_norm(x, ple_gate_norm_g[i]) @ ple_w_gate[i])
        e = rms_norm(p[i] @ ple_w_proj[i], ple_norm_g[i])
        x = x + g * e
    return x


def setup_inputs(seed: int = 0) -> dict:
    key = jax.random.key(seed)
    ks = iter(jax.random.split(key, 40))
    nrm = lambda shape, scale: jax.random.normal(next(ks), shape, jnp.float32) * scale
    gain = lambda shape: 1.0 + nrm(shape, 0.05)
    NA, NB = N_SSD_LAYERS, N_ATT_LAYERS
    dt0 = jnp.exp(jax.random.uniform(next(ks), (NA, 2, SSD_HEADS), jnp.float32,
                                     math.log(1e-3), math.log(1e-1)))
    dt_bias = dt0 + jnp.log(-jnp.expm1(-dt0))
    a_log = jnp.log(jax.random.uniform(next(ks), (NA, 2, SSD_HEADS), jnp.float32, 1.0, 16.0))
    return {
        "x_prompt": nrm((BATCH, SEQ, D_MODEL), 1.0),
        "x_sample": nrm((DEC_BATCH, DEC_SEQ, D_MODEL), 1.0),
        "p_prompt": nrm((DEPTH, BATCH, SEQ, PLE_DIM), 1.0),
        "p_sample": nrm((DEPTH, DEC_BATCH, DEC_SEQ, PLE_DIM), 1.0),
        "pre_norm_g": gain((DEPTH, D_MODEL)),
        "ssd_w_in": nrm((NA, D_MODEL, SSD_IN_COLS), D_MODEL ** -0.5),
        "ssd_conv_w": nrm((NA, SSD_CONV_W, SSD_CONV_CH), SSD_CONV_W ** -0.5),
        "ssd_conv_b": nrm((NA, SSD_CONV_CH), 0.02),
        "ssd_dt_bias": dt_bias,
        "ssd_a_log": a_log,
        "ssd_d_skip": gain((NA, SSD_HEADS)),
        "ssd_norm_g": gain((NA, SSD_INNER)),
        "ssd_w_out": nrm((NA, SSD_INNER, D_MODEL), SSD_INNER ** -0.5),
        "att_w_in": nrm((NB, D_MODEL, ATT_IN_COLS), D_MODEL ** -0.5),
        "att_q_norm_g": gain((NB, ATT_QK_DIM)),
        "att_k_norm_g": gain((NB, ATT_QK_DIM)),
        "att_lam_q1": nrm((NB, ATT_QK_DIM), 0.1),
        "att_lam_k1": nrm((NB, ATT_QK_DIM), 0.1),
        "att_lam_q2": nrm((NB, ATT_QK_DIM), 0.1),
        "att_lam_k2": nrm((NB, ATT_QK_DIM), 0.1),
        "att_sub_norm_g": gain((NB, ATT_V_DIM)),
        "att_w_out": nrm((NB, ATT_WIDTH, D_MODEL), ATT_WIDTH ** -0.5),
        "ple_w_proj": nrm((DEPTH, PLE_DIM, D_MODEL), PLE_DIM ** -0.5),
        "ple_norm_g": gain((DEPTH, D_MODEL)),
        "ple_gate_norm_g": gain((DEPTH, D_MODEL)),
        "ple_w_gate": nrm((DEPTH, D_MODEL, D_MODEL), D_MODEL ** -0.5),
    }


def reference(x_prompt, x_sample, p_prompt, p_sample, pre_norm_g, ssd_w_in, ssd_conv_w,
              ssd_conv_b, ssd_dt_bias, ssd_a_log, ssd_d_skip, ssd_norm_g, ssd_w_out,
              att_w_in, att_q_norm_g, att_k_norm_g, att_lam_q1, att_lam_k1, att_lam_q2,
              att_lam_k2, att_sub_norm_g, att_w_out, ple_w_proj, ple_norm_g,
              ple_gate_norm_g, ple_w_gate):
    y_prompt = trunk(x_prompt, p_prompt, pre_norm_g, ssd_w_in, ssd_conv_w, ssd_conv_b,
                     ssd_dt_bias, ssd_a_log, ssd_d_skip, ssd_norm_g, ssd_w_out, att_w_in,
                     att_q_norm_g, att_k_norm_g, att_lam_q1, att_lam_k1, att_lam_q2,
                     att_lam_k2, att_sub_norm_g, att_w_out, ple_w_proj, ple_norm_g,
                     ple_gate_norm_g, ple_w_gate)
    y_sample = trunk(x_sample, p_sample, pre_norm_g, ssd_w_in, ssd_conv_w, ssd_conv_b,
                     ssd_dt_bias, ssd_a_log, ssd_d_skip, ssd_norm_g, ssd_w_out, att_w_in,
                     att_q_norm_g, att_k_norm_g, att_lam_q1, att_lam_k1, att_lam_q2,
                     att_lam_k2, att_sub_norm_g, att_w_out, ple_w_proj, ple_norm_g,
                     ple_gate_norm_g, ple_w_gate)
    return (y_prompt, y_sample)
```

```python
import math
import numpy as np
import ml_dtypes
from contextlib import ExitStack
import concourse.bass as bass
import concourse.mybir as mybir
from concourse.bass_utils import run_bass_kernel_spmd

F32 = mybir.dt.float32
BF16 = mybir.dt.bfloat16
AF = mybir.ActivationFunctionType
ALU = mybir.AluOpType

NSLOT = 6
D = 1024
PLE = 256
EPS = 1e-6
NEG = -30000.0
class Prog:
    ENG = ["pe", "act", "dve", "pool", "sp"]

    def __init__(self, nc, es, block=None):
        self.nc = nc
        self.block = block
        if block is not None:
            self.handles = {"pe": block.tensor, "act": block.scalar, "dve": block.vector,
                            "pool": block.gpsimd, "sp": block.sync}
        self.items = {e: [] for e in self.ENG}
        self.count = {e: 0 for e in self.ENG}
        self.known = {e: {} for e in self.ENG}
        self.sem = {}
        for e in self.ENG:
            self.sem[("e", e)] = es.enter_context(nc.semaphore("s_" + e))
        self.dmaq = ["sp", "pool", "act"]
        self.dcount = {q: 0 for q in self.dmaq}
        for q in self.dmaq:
            for s in range(NSLOT):
                self.sem[("d", q, s)] = es.enter_context(nc.semaphore(f"d_{q}{s}"))
        self.lastw = {}
        self.readers = {}
        self.ninstr = 0

    def _deps(self, reads, writes):
        deps = {}

        def add(d):
            if d is None:
                return
            k, v = d
            if deps.get(k, 0) < v:
                deps[k] = v
        for r in reads:
            add(self.lastw.get(r))
        for w in writes:
            add(self.lastw.get(w))
            for d in self.readers.get(w, {}).items():
                add(d)
        return deps

    def _emit(self, eng, deps, fn, semkey, inc):
        waits = []
        kn = self.known[eng]
        for k, v in deps.items():
            if k == ("e", "pe") and eng == "pe":
                continue
            if kn.get(k, 0) >= v:
                continue
            kn[k] = v
            waits.append((k, v))
        self._push(eng, (waits, fn, semkey, inc))
        self.ninstr += 1 + len(waits)

    def _push(self, eng, item):
        if self.block is None:
            self.items[eng].append(item)
            return
        waits, fn, semkey, inc = item
        sem = self.sem

        def body(h):
            for k, v in waits:
                h.wait_ge(sem[k], v)
            if fn is not None:
                fn(h).then_inc(sem[semkey], inc)
        self.handles[eng](body)

    def _mark(self, reads, writes, tag):
        for w in writes:
            self.lastw[w] = tag
            self.readers[w] = {}
        for r in reads:
            d = self.readers.setdefault(r, {})
            if d.get(tag[0], 0) < tag[1]:
                d[tag[0]] = tag[1]

    def op(self, eng, fn, reads=(), writes=()):
        deps = self._deps(reads, writes)
        self.count[eng] += 1
        tag = (("e", eng), self.count[eng])
        self._emit(eng, deps, fn, ("e", eng), 1)
        self._mark(reads, writes, tag)

    def dma(self, q, out, in_, reads=(), writes=(), **kw):
        deps = self._deps(reads, writes)
        j = self.dcount[q]
        self.dcount[q] += 1
        slot = j % NSLOT
        key = ("d", q, slot)
        if j >= NSLOT:
            v = 16 * (j // NSLOT)
            if deps.get(key, 0) < v:
                deps[key] = v
        tag = (key, 16 * (j // NSLOT + 1))
        self._emit(q, deps, lambda e: e.dma_start(out=out, in_=in_, **kw), key, 16)
        self._mark(reads, writes, tag)

    def barrier(self):
        deps = {}
        for e in self.ENG:
            if self.count[e]:
                deps[("e", e)] = self.count[e]
        for q in self.dmaq:
            j = self.dcount[q]
            for s in range(NSLOT):
                n = (j - s + NSLOT - 1) // NSLOT if j > s else 0
                if n:
                    deps[("d", q, s)] = 16 * n
        for e in self.ENG:
            waits = []
            kn = self.known[e]
            for k, v in deps.items():
                if k == ("e", "pe") and e == "pe":
                    continue
                if kn.get(k, 0) >= v:
                    continue
                kn[k] = v
                waits.append((k, v))
            if waits:
                self._push(e, (waits, None, None, 0))
                self.ninstr += len(waits)
        self.lastw = {}
        self.readers = {}

    def finish(self):
        self.barrier()

    def replay(self, block):
        nc = self.nc
        handles = {"pe": block.tensor, "act": block.scalar, "dve": block.vector,
                   "pool": block.gpsimd, "sp": block.sync}
        for e in self.ENG:
            items = self.items[e]
            sem = self.sem

            def body(h, items=items):
                for waits, fn, semkey, inc in items:
                    for k, v in waits:
                        h.wait_ge(sem[k], v)
                    if fn is not None:
                        fn(h).then_inc(sem[semkey], inc)
            handles[e](body)


class KB:
    def __init__(self, L, layers, lam_inits):
        self.L = L
        self.layers = layers
        self.lam_inits = lam_inits
        self.NT = L // 128

    def sb(self, st, name, shape, dt):
        self.uid = getattr(self, "uid", 0) + 1
        return st.enter_context(self.nc.sbuf_tensor(f"{name}_{self.uid}", shape, dt))

    def dram_in(self, name, shape, dt=F32):
        return self.nc.dram_tensor(name, list(shape), dt, kind="ExternalInput").ap()

    def scratch(self, name, shape, dt):
        return self.nc.dram_tensor(name, list(shape), dt, kind="Internal").ap()

    def load_w(self, st, name, src, K, N, scale=None):
        P = self.P
        KC = K // 128
        dst = self.sb(st, name, [128, KC, N], BF16)
        srcv = src.rearrange("(c p) n -> p c n", p=128)
        i = 0
        for c in range(KC):
            for n0 in range(0, N, 1024):
                n1 = min(N, n0 + 1024)
                b = self.wcnt % 2
                self.wcnt += 1
                stg = self.wst[b]
                P.dma("sp", stg[:, 0:n1 - n0], srcv[:, c, n0:n1], writes=[f"wst{b}"])
                eng = "pool" if b else "dve"
                if scale is None:
                    P.op(eng, lambda e, stg=stg, c=c, n0=n0, n1=n1: e.tensor_copy(out=dst[:, c, n0:n1], in_=stg[:, 0:n1 - n0]),
                         reads=[f"wst{b}"], writes=[name])
                else:
                    sc, sk = scale
                    P.op(eng, lambda e, stg=stg, c=c, n0=n0, n1=n1: e.tensor_scalar(out=dst[:, c, n0:n1], in0=stg[:, 0:n1 - n0], scalar1=sc[:, c:c + 1], scalar2=None, op0=ALU.mult),
                         reads=[f"wst{b}", sk], writes=[name])
        return dst

    def rstd_from_ss(self, ss_ap, ss_key, out_ap, out_key, n):
        P = self.P
        P.op("act", lambda e: e.activation(out=out_ap, in_=ss_ap, func=AF.Sqrt, scale=1.0 / n, bias=self.epsc[:, 0:1]),
             reads=[ss_key, "epsc"], writes=[out_key])
        P.op("dve", lambda e: e.reciprocal(out=out_ap, in_=out_ap), reads=[out_key], writes=[out_key])

    def norm_transpose(self, xt, xkey, dst, dkey, col0, mask_col=None):
        P = self.P
        P.op("pool", lambda e: e.memset(self.ss[:, 0:1], 0.0), writes=["ss"])
        P.op("dve", lambda e: e.scalar_tensor_tensor(out=self.junk[:, 0:D], in0=xt, scalar=1.0, in1=xt, op0=ALU.mult, op1=ALU.mult, accum_out=self.ss[:, 0:1]),
             reads=[xkey, "ss"], writes=["junk", "ss"])
        self.rstd_from_ss(self.ss[:, 0:1], "ss", self.rs[:, 0:1], "rs", D)
        if mask_col is not None:
            P.op("dve", lambda e: e.tensor_tensor(out=self.rs[:, 0:1], in0=self.rs[:, 0:1], in1=mask_col, op=ALU.mult), reads=["rs", "tm"], writes=["rs"])
        P.op("dve", lambda e: e.tensor_scalar(out=self.hb[:], in0=xt, scalar1=self.rs[:, 0:1], scalar2=None, op0=ALU.mult),
             reads=[xkey, "rs"], writes=["hb"])
        self.transpose_into(self.hb, "hb", 8, dst, dkey, col0)

    def transpose_into(self, src, skey, nchunk, dst, dkey, col0):
        P = self.P
        for c0 in range(0, nchunk, 8):
            n = min(8, nchunk - c0)
            for c in range(n):
                P.op("pe", lambda e, c=c: e.transpose(out=self.pT[:, c, :], in_=src[:, (c0 + c) * 128:(c0 + c + 1) * 128], identity=self.ident[:]),
                     reads=[skey, "ident"], writes=["pT"])
            P.op("act", lambda e, n=n, c0=c0: e.copy(out=dst[:, c0:c0 + n, col0:col0 + 128], in_=self.pT[:, 0:n, :]), reads=["pT"], writes=[dkey])

    def tail(self, li, t, pmix, pmix_key, xin, xout):
        P = self.P
        r0 = t * 128
        P.dma("pool", self.xt[:], xin[r0:r0 + 128, :], reads=[("x", t)], writes=["xt"])
        P.dma("pool", self.pt[:], self.p_d[li, r0:r0 + 128, :], writes=["pt"])
        P.op("dve", lambda e: e.tensor_tensor(out=self.xm[:], in0=pmix.rearrange("p a b -> p (a b)"), in1=self.xt[:], op=ALU.add),
             reads=[pmix_key, "xt"], writes=["xm"])
        self.norm_transpose(self.xm[:], "xm", self.h2T, "h2T", 0)
        P.op("pool", lambda e: e.tensor_copy(out=self.pb[:], in_=self.pt[:]), reads=["pt"], writes=["pb"])
        self.transpose_into(self.pb, "pb", 2, self.ppT, "ppT", 0)
        for half in range(2):
            for c in range(8):
                P.op("pe", lambda e, c=c, half=half: e.matmul(self.pB[:, half, :], self.h2T[:, c, :], self.Wg[:, c, half * 512:(half + 1) * 512], start=(c == 0), stop=(c == 7)),
                     reads=["h2T", "Wg"], writes=["pB"])
        for half in range(2):
            for c in range(2):
                P.op("pe", lambda e, c=c, half=half: e.matmul(self.pC[:, half, :], self.ppT[:, c, :], self.Wp[:, c, half * 512:(half + 1) * 512], start=(c == 0), stop=(c == 1)),
                     reads=["ppT", "Wp"], writes=["pC"])
        P.op("act", lambda e: e.activation(out=self.gs[:], in_=self.pB.rearrange("p a b -> p (a b)"), func=AF.Sigmoid), reads=["pB"], writes=["gs"])
        P.op("pool", lambda e: e.memset(self.ss[:, 1:2], 0.0), writes=["ss2"])
        P.op("act", lambda e: e.activation(out=self.junk[:, 0:D], in_=self.pC.rearrange("p a b -> p (a b)"), func=AF.Square, accum_out=self.ss[:, 1:2]),
             reads=["pC", "ss2"], writes=["junk", "ss2"])
        self.rstd_from_ss(self.ss[:, 1:2], "ss2", self.rs[:, 1:2], "rs2", D)
        P.op("dve", lambda e: e.scalar_tensor_tensor(out=self.ev[:], in0=self.pC.rearrange("p a b -> p (a b)"), scalar=self.rs[:, 1:2], in1=self.gple[:], op0=ALU.mult, op1=ALU.mult),
             reads=["pC", "rs2", "gple"], writes=["ev"])
        P.op("pool", lambda e: e.tensor_tensor(out=self.ev[:], in0=self.ev[:], in1=self.gs[:], op=ALU.mult), reads=["ev", "gs"], writes=["ev"])
        P.op("dve", lambda e: e.tensor_tensor(out=self.ev[:], in0=self.ev[:], in1=self.xm[:], op=ALU.add), reads=["ev", "xm"], writes=["ev"])
        P.dma("sp", xout[r0:r0 + 128, :], self.ev[:], reads=["ev"], writes=[("x", t)])

    def load_tail_weights(self, st, li):
        self.Wg = self.load_w(st, "Wg", self.w["ple_w_gate"][li], D, D, scale=(self.gcols[:, 4 + li, :], "gcols"))
        self.Wp = self.load_w(st, "Wp", self.w["ple_w_proj"][li], PLE, D)
        self.P.dma("pool", self.gple[:], self.w["ple_norm_g"][li].partition_broadcast(128), reads=[], writes=["gple"])

    def att_layer(self, li, j, xin, xout):
        nc, P, L, NT = self.nc, self.P, self.L, self.NT
        w = self.w
        lam_init = self.lam_inits[li]
        NQ = L // 512
        with ExitStack() as st:
            Win = self.load_w(st, "Win", w["att_w_in"][j], D, 4096, scale=(self.gcols[:, li, :], "gcols"))
            hT = self.sb(st, "hT", [128, 8, 512], BF16)
            sq = self.sb(st, "sq", [128, 512], F32)
            rr = self.sb(st, "rr", [128, 512], F32)
            qn = [self.sb(st, f"qn{i}", [128, 512], BF16) for i in range(2)]
            vb = [self.sb(st, f"vb{i}", [128, 1024], BF16) for i in range(2)]
            gqk = self.sb(st, "gqk", [128, 2], F32)
            for m in range(2):
                P.dma("sp", gqk[m * 64:(m + 1) * 64, 0:1], w["att_q_norm_g"][j].rearrange("(d o) -> d o", o=1), writes=["gqk"], allow_slow_non_contiguous=True)
                P.dma("sp", gqk[m * 64:(m + 1) * 64, 1:2], w["att_k_norm_g"][j].rearrange("(d o) -> d o", o=1), writes=["gqk"], allow_slow_non_contiguous=True)
            P.op("dve", lambda e: e.tensor_scalar(out=gqk[:, 0:1], in0=gqk[:, 0:1], scalar1=0.125, scalar2=None, op0=ALU.mult), reads=["gqk"], writes=["gqk"])
            banks = [(self.pA, "pA", 0), (self.pA, "pA", 1), (self.pB, "pB", 0), (self.pB, "pB", 1)]
            bi = 0
            for tt in range(NQ):
                for sub in range(4):
                    t = tt * 4 + sub
                    P.dma("pool", self.xt[:], xin[t * 128:(t + 1) * 128, :], reads=[("x", t)], writes=["xt"])
                    self.norm_transpose(self.xt[:], "xt", hT, "hT", sub * 128, mask_col=self.tm[:, t:t + 1])
                c0 = tt * 512
                for cg in range(16):
                    pt_, pk, ph = banks[bi % 4]
                    pk = pk + str(ph)
                    bi += 1
                    for c in range(8):
                        P.op("pe", lambda e, c=c, cg=cg, pt_=pt_, ph=ph: e.matmul(pt_[:, ph, :], Win[:, c, cg * 128:(cg + 1) * 128], hT[:, c, :], start=(c == 0), stop=(c == 7)),
                             reads=["Win", "hT"], writes=[pk])
                    P.op("act", lambda e, pt_=pt_, ph=ph: e.activation(out=sq[:], in_=pt_[:, ph, :], func=AF.Square), reads=[pk], writes=["sq"])
                    P.op("pe", lambda e: e.matmul(self.pC[:, 0, :], self.blk1[:], sq[:], start=True, stop=True), reads=["sq", "blk1"], writes=["pC0"])
                    P.op("act", lambda e: e.activation(out=rr[:], in_=self.pC[:, 0, :], func=AF.Sqrt, scale=1.0 / 64, bias=self.epsc[:, 0:1]), reads=["pC0", "epsc"], writes=["rr"])
                    P.op("dve", lambda e: e.reciprocal(out=rr[:], in_=rr[:]), reads=["rr"], writes=["rr"])
                    qb = qn[cg % 2]
                    gc = gqk[:, 0:1] if cg < 8 else gqk[:, 1:2]
                    P.op("dve", lambda e, pt_=pt_, ph=ph, qb=qb, gc=gc: e.scalar_tensor_tensor(out=qb[:], in0=pt_[:, ph, :], scalar=gc, in1=rr[:], op0=ALU.mult, op1=ALU.mult),
                         reads=[pk, "gqk", "rr"], writes=[f"qn{cg % 2}"])
                    dst = self.qT_d if cg < 8 else self.kT_d
                    P.dma("sp", dst[cg % 8, :, c0:c0 + 512], qb[:], reads=[f"qn{cg % 2}"], writes=["qkT_d"])
                for sub in range(4):
                    t = tt * 4 + sub
                    for half in range(2):
                        pt_, pk, ph = banks[bi % 4]
                        pk = pk + str(ph)
                        bi += 1
                        for c in range(8):
                            P.op("pe", lambda e, c=c, half=half, sub=sub, pt_=pt_, ph=ph: e.matmul(pt_[:, ph, :], hT[:, c, sub * 128:(sub + 1) * 128], Win[:, c, 2048 + half * 512:2048 + (half + 1) * 512], start=(c == 0), stop=(c == 7)),
                                 reads=["Win", "hT"], writes=[pk])
                        P.op("act", lambda e, pt_=pt_, ph=ph, half=half, t=t: e.copy(out=vb[t % 2][:, half * 512:(half + 1) * 512], in_=pt_[:, ph, :]), reads=[pk], writes=[f"vb{t % 2}"])
                    P.dma("sp", self.v_d[t * 128:(t + 1) * 128, :], vb[t % 2][:], reads=[f"vb{t % 2}"], writes=["v_d"])
                for cg in range(8):
                    pt_, pk, ph = banks[bi % 4]
                    pk = pk + str(ph)
                    bi += 1
                    for c in range(8):
                        P.op("pe", lambda e, c=c, cg=cg, pt_=pt_, ph=ph: e.matmul(pt_[:, ph, :], Win[:, c, 3072 + cg * 128:3072 + (cg + 1) * 128], hT[:, c, :], start=(c == 0), stop=(c == 7)),
                             reads=["Win", "hT"], writes=[pk])
                    qb = qn[cg % 2]
                    P.op("act", lambda e, pt_=pt_, ph=ph, qb=qb: e.activation(out=qb[:], in_=pt_[:, ph, :], func=AF.Silu), reads=[pk], writes=[f"qn{cg % 2}"])
                    P.dma("sp", self.sg_d[cg * 128:(cg + 1) * 128, c0:c0 + 512], qb[:], reads=[f"qn{cg % 2}"], writes=["sg_d"])
            P.barrier()
        with ExitStack() as st:
            K = [self.sb(st, f"K{m}", [69, L], BF16) for m in range(2)]
            V = self.sb(st, "V", [128, NT, 128], BF16)
            Q = {(v, m, b): self.sb(st, f"Q{v}{m}{b}", [69, 512], BF16) for v in "ABC" for m in range(2) for b in range(2)}
            sgt = [self.sb(st, f"sgt{b}", [128, 512], BF16) for b in range(2)]
            Pt = [self.sb(st, f"Pt{b}", [128, 2, 512], BF16) for b in range(2)]
            Sd = self.sb(st, "Sd", [128, 2, 512], F32)
            acc = self.sb(st, "acc", [128, 2, 512], F32)
            rd = self.sb(st, "rd", [128, 2, 512], F32)
            o1 = self.sb(st, "o1", [128, 512], F32)
            t2 = self.sb(st, "t2", [128, 512], F32)
            osq = self.sb(st, "osq", [128, 512], F32)
            r2 = self.sb(st, "r2", [128, 512], F32)
            ogb = [self.sb(st, f"ogb{b}", [128, 512], BF16) for b in range(2)]
            dt4 = self.sb(st, "dt4", [128, 4, 512], F32)
            lamv = self.sb(st, "lamv", [128, 4, 64], F32)
            lams = self.sb(st, "lams", [128, 4], F32)
            gsub = self.sb(st, "gsub", [128, 1], F32)
            for i, nm in enumerate(["att_lam_q1", "att_lam_k1", "att_lam_q2", "att_lam_k2"]):
                P.dma("sp", lamv[:, i, :], w[nm][j].partition_broadcast(128), writes=["lamv"])
            P.op("pool", lambda e: e.memset(lams[:], 0.0), writes=["lams"])
            P.op("dve", lambda e: e.scalar_tensor_tensor(out=self.junk[:, 0:64], in0=lamv[:, 0, :], scalar=1.0, in1=lamv[:, 1, :], op0=ALU.mult, op1=ALU.mult, accum_out=lams[:, 0:1]),
                 reads=["lamv", "lams"], writes=["junk", "lams"])
            P.op("dve", lambda e: e.scalar_tensor_tensor(out=self.junk[:, 0:64], in0=lamv[:, 2, :], scalar=1.0, in1=lamv[:, 3, :], op0=ALU.mult, op1=ALU.mult, accum_out=lams[:, 1:2]),
                 reads=["lamv", "lams", "junk"], writes=["junk", "lams"])
            P.op("act", lambda e: e.activation(out=lams[:, 0:2], in_=lams[:, 0:2], func=AF.Exp), reads=["lams"], writes=["lams"])
            P.op("dve", lambda e: e.scalar_tensor_tensor(out=lams[:, 2:3], in0=lams[:, 1:2], scalar=-lam_init, in1=lams[:, 0:1], op0=ALU.add, op1=ALU.subtract),
                 reads=["lams"], writes=["lams"])
            P.dma("sp", gsub[:], w["att_sub_norm_g"][j].rearrange("(d o) -> d o", o=1), writes=["gsub"], allow_slow_non_contiguous=True)
            P.op("dve", lambda e: e.tensor_scalar(out=gsub[:], in0=gsub[:], scalar1=1.0 - lam_init, scalar2=None, op0=ALU.mult), reads=["gsub"], writes=["gsub"])
            nKT = L // 128
            for h in range(8):
                for m in range(2):
                    P.dma("sp", K[m][0:64, :], self.kT_d[h, m * 64:(m + 1) * 64, :], reads=["qkT_d"], writes=[f"K{m}"])
                    P.dma("pool", K[m][64:69, :], self.kext[h], writes=[f"K{m}"])
                P.dma("sp", V[:], self.v_d[:, h * 128:(h + 1) * 128].rearrange("(t p) e -> p t e", p=128), reads=["v_d"], writes=["V"])
                P.dma("pool", dt4[:], self.dtab[h].rearrange("j p q -> p j q"), writes=["dt4"])
                for qt in range(NQ):
                    b = qt % 2
                    q0 = qt * 512
                    for m in range(2):
                        for vi, v in enumerate("ABC"):
                            P.dma("pool" if m else "sp", Q[(v, m, b)][0:64, :], self.qT_d[h, m * 64:(m + 1) * 64, q0:q0 + 512], reads=["qkT_d"], writes=[f"Q{v}{m}{b}"])
                            P.dma("pool" if m else "sp", Q[(v, m, b)][64:69, :], self.qext[vi, h, :, q0:q0 + 512], writes=[f"Q{v}{m}{b}"])
                    P.dma("pool", sgt[b][:], self.sg_d[h * 128:(h + 1) * 128, q0:q0 + 512], reads=["sg_d"], writes=[f"sgt{b}"])
                    for kt in range(nKT):
                        if (kt + 1) * 128 <= q0:
                            v = "A"
                        elif kt * 128 >= q0 + 512:
                            v = "B"
                        else:
                            v = "C"
                        S, sk = (self.pA, "pA") if kt % 2 == 0 else (self.pB, "pB")
                        pb_ = kt % 2
                        for m in range(2):
                            P.op("pe", lambda e, m=m, S=S, kt=kt, v=v: e.matmul(S[:, m, :], K[m][:, kt * 128:(kt + 1) * 128], Q[(v, m, b)][:], start=True, stop=True),
                                 reads=[f"K{m}", f"Q{v}{m}{b}"], writes=[sk])
                        if v == "C":
                            jj = kt - qt * 4
                            P.op("dve", lambda e, S=S, jj=jj: e.tensor_tensor(out=Sd[:], in0=S[:], in1=dt4[:, jj, :].unsqueeze(1).to_broadcast([128, 2, 512]), op=ALU.add),
                                 reads=[sk, "dt4"], writes=["Sd"])
                            P.op("act", lambda e, pb_=pb_: e.activation(out=Pt[pb_][:], in_=Sd[:], func=AF.Exp), reads=["Sd"], writes=[f"Pt{pb_}"])
                        else:
                            P.op("act", lambda e, S=S, pb_=pb_: e.activation(out=Pt[pb_][:], in_=S[:], func=AF.Exp), reads=[sk], writes=[f"Pt{pb_}"])
                        for m in range(2):
                            P.op("pe", lambda e, m=m, kt=kt, pb_=pb_: e.matmul(self.pC[:, m, :], V[:, kt, :], Pt[pb_][:, m, :], start=(kt == 0), stop=(kt == nKT - 1)),
                                 reads=["V", f"Pt{pb_}"], writes=["pC"])
                        if kt == 0:
                            P.op("dve", lambda e, pb_=pb_: e.tensor_copy(out=acc[:], in_=Pt[pb_][:]), reads=[f"Pt{pb_}"], writes=["acc"])
                        else:
                            P.op("dve", lambda e, pb_=pb_: e.tensor_tensor(out=acc[:], in0=acc[:], in1=Pt[pb_][:], op=ALU.add), reads=[f"Pt{pb_}", "acc"], writes=["acc"])
                    for m in range(2):
                        P.op("pe", lambda e, m=m: e.matmul(self.pA[:, m, :], self.ones_f[:], acc[:, m, :], start=True, stop=True), reads=["acc", "ones_f"], writes=["pA"])
                    P.op("dve", lambda e: e.tensor_scalar(out=rd[:], in0=self.pA[:], scalar1=1e-30, scalar2=None, op0=ALU.add), reads=["pA"], writes=["rd"])
                    P.op("dve", lambda e: e.reciprocal(out=rd[:], in_=rd[:]), reads=["rd"], writes=["rd"])
                    P.op("dve", lambda e: e.tensor_tensor(out=o1[:], in0=self.pC[:, 0, :], in1=rd[:, 0, :], op=ALU.mult), reads=["pC", "rd"], writes=["o1"])
                    P.op("dve", lambda e: e.tensor_tensor(out=t2[:], in0=self.pC[:, 1, :], in1=rd[:, 1, :], op=ALU.mult), reads=["pC", "rd"], writes=["t2"])
                    P.op("dve", lambda e: e.scalar_tensor_tensor(out=o1[:], in0=t2[:], scalar=lams[:, 2:3], in1=o1[:], op0=ALU.mult, op1=ALU.add), reads=["t2", "o1", "lams"], writes=["o1"])
                    P.op("pool", lambda e: e.tensor_tensor(out=osq[:], in0=o1[:], in1=o1[:], op=ALU.mult), reads=["o1"], writes=["osq"])
                    P.op("pe", lambda e: e.matmul(self.pB[:, 0, :], self.ones_f[:], osq[:], start=True, stop=True), reads=["osq", "ones_f"], writes=["pB"])
                    P.op("act", lambda e: e.activation(out=r2[:], in_=self.pB[:, 0, :], func=AF.Sqrt, scale=1.0 / 128, bias=self.epsc[:, 0:1]), reads=["pB", "epsc"], writes=["r2"])
                    P.op("dve", lambda e: e.reciprocal(out=r2[:], in_=r2[:]), reads=["r2"], writes=["r2"])
                    P.op("dve", lambda e: e.scalar_tensor_tensor(out=o1[:], in0=o1[:], scalar=gsub[:, 0:1], in1=r2[:], op0=ALU.mult, op1=ALU.mult), reads=["o1", "gsub", "r2"], writes=["o1"])
                    P.op("dve", lambda e, b=b: e.tensor_tensor(out=ogb[b][:], in0=o1[:], in1=sgt[b][:], op=ALU.mult), reads=["o1", f"sgt{b}"], writes=[f"ogb{b}"])
                    P.dma("sp", self.og_d[h * 128:(h + 1) * 128, q0:q0 + 512], ogb[b][:], reads=[f"ogb{b}"], writes=["og_d"])
            P.barrier()
        with ExitStack() as st:
            Wout = self.load_w(st, "Wout", w["att_w_out"][j], D, D)
            self.load_tail_weights(st, li)
            ogt = [self.sb(st, f"ogt{b}", [128, 8, 128], BF16) for b in range(2)]
            for t in range(NT):
                b = t % 2
                P.dma("sp", ogt[b][:], self.og_d[:, t * 128:(t + 1) * 128].rearrange("(h e) t -> e h t", e=128), reads=["og_d"], writes=[f"ogt{b}"])
                for half in range(2):
                    for h in range(8):
                        P.op("pe", lambda e, h=h, half=half, b=b: e.matmul(self.pA[:, half, :], ogt[b][:, h, :], Wout[:, h, half * 512:(half + 1) * 512], start=(h == 0), stop=(h == 7)),
                             reads=[f"ogt{b}", "Wout"], writes=["pA"])
                self.tail(li, t, self.pA, "pA", xin, xout)
            P.barrier()

    def ssd_layer(self, li, j, xin, xout):
        nc, P, L, NT = self.nc, self.P, self.L, self.NT
        w = self.w
        NQ = L // 512
        with ExitStack() as st:
            Win = self.load_w(st, "Win", w["ssd_w_in"][j], D, 6208, scale=(self.gcols[:, li, :], "gcols"))
            hT = self.sb(st, "hT", [128, 8, 512], BF16)
            rawo = [self.sb(st, f"rawo{i}", [128, 512], F32) for i in range(2)]
            zs = self.sb(st, "zs", [128, 2048], F32)
            dtb = self.sb(st, "dtb", [128, 64], F32)
            dts = self.sb(st, "dts", [128, 64], F32)
            onec = self.sb(st, "onec", [128, 1], F32)
            P.op("pool", lambda e: e.memset(onec[:], 1.0), writes=["onec"])
            P.dma("sp", dtb[:], w["ssd_dt_bias"][j].partition_broadcast(128), writes=["dtb"])
            banks = [(self.pA, "pA", 0), (self.pA, "pA", 1), (self.pB, "pB", 0), (self.pB, "pB", 1)]
            bi = 0
            for tt in range(NQ):
                c0 = tt * 512
                for sub in range(4):
                    t = tt * 4 + sub
                    P.dma("pool", self.xt[:], xin[t * 128:(t + 1) * 128, :], reads=[("x", t)], writes=["xt"])
                    self.norm_transpose(self.xt[:], "xt", hT, "hT", sub * 128, mask_col=self.tm[:, t:t + 1])
                for cg in range(32):
                    pt_, pk, ph = banks[bi % 4]
                    pk = pk + str(ph)
                    bi += 1
                    for c in range(8):
                        P.op("pe", lambda e, c=c, cg=cg, pt_=pt_, ph=ph: e.matmul(pt_[:, ph, :], Win[:, c, 2048 + cg * 128:2048 + (cg + 1) * 128], hT[:, c, :], start=(c == 0), stop=(c == 7)),
                             reads=["Win", "hT"], writes=[pk])
                    ro = rawo[cg % 2]
                    P.op("act", lambda e, pt_=pt_, ph=ph, ro=ro: e.copy(out=ro[:], in_=pt_[:, ph, :]), reads=[pk], writes=[f"rawo{cg % 2}"])
                    P.dma("sp", self.xbc_d[cg * 128:(cg + 1) * 128, c0:c0 + 512], ro[:], reads=[f"rawo{cg % 2}"], writes=["xbc_d"])
                for sub in range(4):
                    t = tt * 4 + sub
                    for qq in range(4):
                        pt_, pk, ph = banks[bi % 4]
                        pk = pk + str(ph)
                        bi += 1
                        for c in range(8):
                            P.op("pe", lambda e, c=c, qq=qq, sub=sub, pt_=pt_, ph=ph: e.matmul(pt_[:, ph, :], hT[:, c, sub * 128:(sub + 1) * 128], Win[:, c, qq * 512:(qq + 1) * 512], start=(c == 0), stop=(c == 7)),
                                 reads=["Win", "hT"], writes=[pk])
                        P.op("act", lambda e, pt_=pt_, ph=ph, qq=qq: e.activation(out=zs[:, qq * 512:(qq + 1) * 512], in_=pt_[:, ph, :], func=AF.Silu), reads=[pk], writes=["zs"])
                    P.dma("sp", self.z_d[t * 128:(t + 1) * 128, :], zs[:], reads=["zs"], writes=["z_d"])
                    for c in range(8):
                        P.op("pe", lambda e, c=c, sub=sub: e.matmul(self.pS[:, 0:64], hT[:, c, sub * 128:(sub + 1) * 128], Win[:, c, 6144:6208], start=(c == 0), stop=(c == 7)),
                             reads=["Win", "hT"], writes=["pS"])
                    P.op("dve", lambda e: e.tensor_tensor(out=dts[:], in0=self.pS[:, 0:64], in1=dtb[:], op=ALU.add), reads=["pS", "dtb"], writes=["dts"])
                    P.op("act", lambda e: e.activation(out=dts[:], in_=dts[:], func=AF.Exp), reads=["dts"], writes=["dts"])
                    P.op("act", lambda e: e.activation(out=dts[:], in_=dts[:], func=AF.Ln, bias=onec[:, 0:1]), reads=["dts", "onec"], writes=["dts"])
                    P.op("dve", lambda e, t=t: e.tensor_scalar(out=dts[:], in0=dts[:], scalar1=self.tm[:, t:t + 1], scalar2=None, op0=ALU.mult), reads=["dts", "tm"], writes=["dts"])
                    P.dma("sp", self.dt_d[t * 128:(t + 1) * 128, :], dts[:], reads=["dts"], writes=["dt_d"])
            P.barrier()
        with ExitStack() as st:
            Wout = self.load_w(st, "Wout", w["ssd_w_out"][j], 2048, D, scale=None)
            ng = self.sb(st, "ng", [128, 16], F32)
            P.dma("sp", ng[:], w["ssd_norm_g"][j].rearrange("(c p) -> p c", p=128), writes=["ng"], allow_slow_non_contiguous=True)
            for c in range(16):
                P.op("pool", lambda e, c=c: e.tensor_scalar(out=self.junk[:, 0:D], in0=Wout[:, c, :], scalar1=ng[:, c:c + 1], scalar2=None, op0=ALU.mult), reads=["Wout", "ng"], writes=["junk"])
                P.op("pool", lambda e, c=c: e.tensor_copy(out=Wout[:, c, :], in_=self.junk[:, 0:D]), reads=["junk"], writes=["Wout"])
            self.load_tail_weights(st, li)
            cw = self.sb(st, "cw", [128, 32, 5], F32)
            cb = self.sb(st, "cb", [128, 32], F32)
            for k in range(5):
                for q4 in range(4):
                    P.dma("sp", cw[:, q4 * 8:(q4 + 1) * 8, k:k + 1], w["ssd_conv_w"][j, k, q4 * 1024:(q4 + 1) * 1024].rearrange("(g p o) -> p g o", p=128, o=1), writes=["cw"], allow_slow_non_contiguous=True)
            for q4 in range(4):
                P.dma("sp", cb[:, q4 * 8:(q4 + 1) * 8], w["ssd_conv_b"][j, q4 * 1024:(q4 + 1) * 1024].rearrange("(g p) -> p g", p=128), writes=["cb"], allow_slow_non_contiguous=True)
            abc = self.sb(st, "abc", [128, 64], F32)
            Dbc = self.sb(st, "Dbc", [128, 32], F32)
            P.dma("sp", abc[:], w["ssd_a_log"][j].partition_broadcast(128), writes=["abc"])
            P.dma("sp", Dbc[:], w["ssd_d_skip"][j].partition_broadcast(128), writes=["Dbc"])
            P.op("act", lambda e: e.activation(out=abc[:], in_=abc[:], func=AF.Exp), reads=["abc"], writes=["abc"])
            P.op("dve", lambda e: e.tensor_scalar(out=abc[:], in0=abc[:], scalar1=-1.0, scalar2=None, op0=ALU.mult), reads=["abc"], writes=["abc"])
            raw = [self.sb(st, f"raw{i}", [128, 516], F32) for i in range(2)]
            cacc = [self.sb(st, f"cacc{i}", [128, 512], F32) for i in range(2)]
            XT = self.sb(st, "XT", [128, 16, 512], BF16)
            BT = self.sb(st, "BT", [128, 8, 512], BF16)
            CT = self.sb(st, "CT", [128, 8, 512], BF16)
            xtok = self.sb(st, "xtok", [128, 16, 128], BF16)
            btok = self.sb(st, "btok", [128, 8, 128], BF16)
            dtc = self.sb(st, "dtc", [128, 32], F32)
            dta = self.sb(st, "dta", [128, 32], F32)
            cs = self.sb(st, "cs", [128, 32], F32)
            Ee = self.sb(st, "Ee", [128, 32], F32)
            dd = self.sb(st, "dd", [128, 32], F32)
            dtot = self.sb(st, "dtot", [128, 32], F32)
            wdt = self.sb(st, "wdt", [128, 32], F32)
            csT = self.sb(st, "csT", [32, 128], F32)
            ncsT = self.sb(st, "ncsT", [32, 128], F32)
            xdt = self.sb(st, "xdt", [128, 32, 64], BF16)
            xdd = self.sb(st, "xdd", [128, 32, 64], BF16)
            cbs = self.sb(st, "cbs", [128, 128], F32)
            Lt = self.sb(st, "Lt", [128, 4, 128], F32)
            Mt = self.sb(st, "Mt", [128, 4, 128], BF16)
            tmpo = self.sb(st, "tmpo", [128, 4, 64], F32)
            ych = self.sb(st, "ych", [128, 2048], F32)
            yft = self.sb(st, "yft", [128, 2048], F32)
            ynb = self.sb(st, "ynb", [128, 2048], BF16)
            ynT = self.sb(st, "ynT", [128, 16, 128], BF16)
            Hf = self.sb(st, "Hf", [128, 8, 256], F32)
            Hb = self.sb(st, "Hb", [128, 8, 256], BF16)

            def prep_tile(tt):
                t0 = tt * 512
                lo = max(0, t0 - 2)
                hi = min(L, t0 + 514)
                for cg in range(32):
                    b = cg % 2
                    eng = "dve"
                    rw = raw[b]
                    if lo > t0 - 2:
                        P.op(eng, lambda e, rw=rw: e.memset(rw[:, 0:2], 0.0), writes=[f"raw{b}"])
                    if hi < t0 + 514:
                        P.op(eng, lambda e, rw=rw: e.memset(rw[:, 514:516], 0.0), writes=[f"raw{b}"])
                    P.dma("sp" if b == 0 else "act", rw[:, lo - (t0 - 2):hi - (t0 - 2)], self.xbc_d[cg * 128:(cg + 1) * 128, lo:hi], reads=["xbc_d"], writes=[f"raw{b}"])
                    ca = cacc[b]
                    P.op(eng, lambda e, rw=rw, ca=ca, cg=cg: e.tensor_scalar(out=ca[:], in0=rw[:, 0:512], scalar1=cw[:, cg, 0:1], scalar2=None, op0=ALU.mult),
                         reads=[f"raw{b}", "cw"], writes=[f"cacc{b}"])
                    for k in range(1, 5):
                        P.op(eng, lambda e, rw=rw, ca=ca, cg=cg, k=k: e.scalar_tensor_tensor(out=ca[:], in0=rw[:, k:k + 512], scalar=cw[:, cg, k:k + 1], in1=ca[:], op0=ALU.mult, op1=ALU.add),
                             reads=[f"raw{b}", "cw", f"cacc{b}"], writes=[f"cacc{b}"])
                    if cg < 16:
                        dst, dk = XT[:, cg, :], "XT"
                    elif cg < 24:
                        dst, dk = BT[:, cg - 16, :], "BT"
                    else:
                        dst, dk = CT[:, cg - 24, :], "CT"
                    P.op("act", lambda e, ca=ca, cg=cg, dst=dst: e.activation(out=dst, in_=ca[:], func=AF.Silu, bias=cb[:, cg:cg + 1]), reads=[f"cacc{b}", "cb"], writes=[dk])

            def chunk(t, l0, bwd):
                dcol = 32 if bwd else 0
                Tm = self.Lo_f if bwd else self.U_f
                tk = "Lo_f" if bwd else "U_f"
                mk, mkk = (self.maskB, "maskB") if bwd else (self.maskF, "maskF")
                for c0_ in (0, 8):
                    for c in range(8):
                        P.op("pe", lambda e, c=c, c0_=c0_: e.transpose(out=self.pT[:, c, :], in_=XT[:, c0_ + c, l0:l0 + 128], identity=self.ident[:]), reads=["XT", "ident"], writes=["pT"])
                    P.op("act", lambda e, c0_=c0_: e.copy(out=xtok[:, c0_:c0_ + 8, :], in_=self.pT[:]), reads=["pT"], writes=["xtok"])
                for c in range(8):
                    P.op("pe", lambda e, c=c: e.transpose(out=self.pT[:, c, :], in_=BT[:, c, l0:l0 + 128], identity=self.ident[:]), reads=["BT", "ident"], writes=["pT"])
                P.op("act", lambda e: e.copy(out=btok[:], in_=self.pT[:]), reads=["pT"], writes=["btok"])
                P.dma("pool", dtc[:], self.dt_d[t * 128:(t + 1) * 128, dcol:dcol + 32], reads=["dt_d"], writes=["dtc"])
                P.op("dve", lambda e: e.tensor_tensor(out=dta[:], in0=dtc[:], in1=abc[:, dcol:dcol + 32], op=ALU.mult), reads=["dtc", "abc"], writes=["dta"])
                P.op("pe", lambda e: e.matmul(self.pS[:, 0:32], Tm[:], dta[:], start=True, stop=True), reads=[tk, "dta"], writes=["pS"])
                P.op("pe", lambda e: e.matmul(self.pS[:, 32:64], self.ones_f[:], dta[:], start=True, stop=True), reads=["ones_f", "dta"], writes=["pS"])
                P.op("act", lambda e: e.copy(out=cs[:], in_=self.pS[:, 0:32]), reads=["pS"], writes=["cs"])
                P.op("act", lambda e: e.activation(out=Ee[:], in_=self.pS[:, 0:32], func=AF.Exp), reads=["pS"], writes=["Ee"])
                P.op("act", lambda e: e.activation(out=dtot[:], in_=self.pS[:, 32:64], func=AF.Exp), reads=["pS"], writes=["dtot"])
                P.op("dve", lambda e: e.tensor_tensor(out=dd[:], in0=self.pS[:, 32:64], in1=cs[:], op=ALU.subtract), reads=["pS", "cs"], writes=["dd"])
                P.op("act", lambda e: e.activation(out=dd[:], in_=dd[:], func=AF.Exp), reads=["dd"], writes=["dd"])
                P.op("dve", lambda e: e.tensor_tensor(out=wdt[:], in0=dd[:], in1=dtc[:], op=ALU.mult), reads=["dd", "dtc"], writes=["wdt"])
                P.op("pe", lambda e: e.transpose(out=self.pS[0:32, 128:256], in_=cs[:], identity=self.ident_f[:]), reads=["cs", "ident_f"], writes=["pS"])
                P.op("act", lambda e: e.copy(out=csT[:], in_=self.pS[0:32, 128:256]), reads=["pS"], writes=["csT"])
                P.op("dve", lambda e: e.tensor_scalar(out=ncsT[:], in0=self.pS[0:32, 128:256], scalar1=-1.0, scalar2=None, op0=ALU.mult), reads=["pS"], writes=["ncsT"])
                xv = xtok[:].rearrange("p c (r d) -> p (c r) d", d=64)
                P.op("dve", lambda e: e.tensor_tensor(out=xdt[:], in0=xv, in1=dtc[:].unsqueeze(2).to_broadcast([128, 32, 64]), op=ALU.mult), reads=["xtok", "dtc"], writes=["xdt"])
                P.op("pool", lambda e: e.tensor_tensor(out=xdd[:], in0=xv, in1=wdt[:].unsqueeze(2).to_broadcast([128, 32, 64]), op=ALU.mult), reads=["xtok", "wdt"], writes=["xdd"])
                seg = self.pA[:, 0, :]
                pY = self.pA[:, 1, 0:256]
                pO = self.pA[:, 1, 256:512]
                pH = self.pB[:, 0, 0:256]
                pcb = self.pB[:, 0, 256:384]
                for g in range(8):
                    P.op("pe", lambda e, g=g: e.matmul(pcb, BT[:, g, l0:l0 + 128], CT[:, g, l0:l0 + 128], start=True, stop=True), reads=["BT", "CT"], writes=["pB"])
                    P.op("act", lambda e: e.copy(out=cbs[:], in_=pcb), reads=["pB"], writes=["cbs"])
                    P.op("pe", lambda e, g=g: e.matmul(seg, ncsT[:], self.ident_f[0:32, 4 * g:4 * g + 4].unsqueeze(2).to_broadcast([32, 4, 128]), start=True, stop=False), reads=["ncsT", "ident_f"], writes=["pA"])
                    P.op("pe", lambda e: e.matmul(seg, self.negI[:], mk[:].rearrange("k r s -> k (r s)"), start=False, stop=False), reads=["negI", mkk], writes=["pA"])
                    for r in range(4):
                        P.op("pe", lambda e, g=g, r=r: e.matmul(seg[:, r * 128:(r + 1) * 128], self.ident_f[0:32, 4 * g + r:4 * g + r + 1].to_broadcast([32, 128]), csT[:], start=False, stop=(r == 3)), reads=["csT", "ident_f"], writes=["pA"])
                    P.op("act", lambda e: e.activation(out=Lt[:].rearrange("p r s -> p (r s)"), in_=seg, func=AF.Exp), reads=["pA"], writes=["Lt"])
                    P.op("dve", lambda e: e.tensor_tensor(out=Mt[:], in0=Lt[:], in1=cbs[:].unsqueeze(1).to_broadcast([128, 4, 128]), op=ALU.mult), reads=["Lt", "cbs"], writes=["Mt"])
                    for r in range(4):
                        P.op("pe", lambda e, g=g, r=r: e.matmul(pY[:, r * 64:(r + 1) * 64], Mt[:, r, :], xdt[:, 4 * g + r, :], start=True, stop=True), reads=["Mt", "xdt"], writes=["pA"])
                    P.op("pe", lambda e, g=g: e.matmul(pO, CT[:, g, l0:l0 + 128], Hb[:, g, :], start=True, stop=True), reads=["CT", "Hb"], writes=["pA"])
                    P.op("dve", lambda e, g=g: e.tensor_tensor(out=tmpo[:], in0=pO.rearrange("p (r d) -> p r d", d=64), in1=Ee[:, 4 * g:4 * g + 4].unsqueeze(2).to_broadcast([128, 4, 64]), op=ALU.mult), reads=["pA", "Ee"], writes=["tmpo"])
                    P.op("dve", lambda e, g=g: e.tensor_tensor(out=ych[:, g * 256:(g + 1) * 256], in0=pY, in1=tmpo[:].rearrange("p r d -> p (r d)"), op=ALU.add), reads=["pA", "tmpo"], writes=["ych"])
                    P.op("pe", lambda e, g=g: e.matmul(pH, btok[:, g, :], xdd[:, 4 * g:4 * g + 4, :].rearrange("p r d -> p (r d)"), start=True, stop=True), reads=["btok", "xdd"], writes=["pB"])
                    P.op("pool", lambda e, g=g: e.tensor_tensor(out=Hf[:, g, :].rearrange("p (r d) -> p r d", d=64), in0=Hf[:, g, :].rearrange("p (r d) -> p r d", d=64), in1=dtot[:, 4 * g:4 * g + 4].unsqueeze(2).to_broadcast([128, 4, 64]), op=ALU.mult), reads=["Hf", "dtot", "Hb"], writes=["Hf"])
                    P.op("dve", lambda e, g=g: e.tensor_tensor(out=Hf[:, g, :], in0=pH, in1=Hf[:, g, :], op=ALU.add), reads=["pB", "Hf"], writes=["Hf"])
                    P.op("act", lambda e, g=g: e.copy(out=Hb[:, g, :], in_=Hf[:, g, :]), reads=["Hf"], writes=["Hb"])

            for bwd in (False, True):
                P.op("pool", lambda e: e.memset(Hf[:].rearrange("p g d -> p (g d)"), 0.0), reads=["Hb"], writes=["Hf"])
                P.op("dve", lambda e: e.memset(Hb[:].rearrange("p g d -> p (g d)"), 0.0), writes=["Hb"])
                tts = range(NQ - 1, -1, -1) if bwd else range(NQ)
                for tt in tts:
                    prep_tile(tt)
                    subs = range(3, -1, -1) if bwd else range(4)
                    for sub in subs:
                        t = tt * 4 + sub
                        chunk(t, sub * 128, bwd)
                        if not bwd:
                            P.dma("sp", self.yf_d[t * 128:(t + 1) * 128, :], ych[:], reads=["ych"], writes=["yf_d"])
                            continue
                        P.dma("sp", yft[:], self.yf_d[t * 128:(t + 1) * 128, :], reads=["yf_d"], writes=["yft"])
                        P.op("dve", lambda e: e.tensor_tensor(out=ych[:], in0=ych[:], in1=yft[:], op=ALU.add), reads=["ych", "yft"], writes=["ych"])
                        xv = xtok[:].rearrange("p c (r d) -> p (c r) d", d=64)
                        P.op("pool", lambda e, xv=xv: e.tensor_tensor(out=yft[:].rearrange("p (r d) -> p r d", d=64), in0=xv, in1=Dbc[:].unsqueeze(2).to_broadcast([128, 32, 64]), op=ALU.mult), reads=["xtok", "Dbc", "ych"], writes=["yft"])
                        P.op("dve", lambda e: e.tensor_tensor(out=ych[:], in0=ych[:], in1=yft[:], op=ALU.add), reads=["ych", "yft"], writes=["ych"])
                        P.dma("sp", yft[:], self.z_d[t * 128:(t + 1) * 128, :], reads=["z_d", "ych"], writes=["yft"])
                        P.op("dve", lambda e: e.tensor_tensor(out=ych[:], in0=ych[:], in1=yft[:], op=ALU.mult), reads=["ych", "yft"], writes=["ych"])
                        P.op("pool", lambda e: e.memset(self.ss[:, 2:3], 0.0), writes=["ss3"])
                        P.op("dve", lambda e: e.scalar_tensor_tensor(out=yft[:], in0=ych[:], scalar=1.0, in1=ych[:], op0=ALU.mult, op1=ALU.mult, accum_out=self.ss[:, 2:3]), reads=["ych", "ss3"], writes=["yft", "ss3"])
                        self.rstd_from_ss(self.ss[:, 2:3], "ss3", self.rs[:, 2:3], "rs3", 2048)
                        P.op("dve", lambda e: e.tensor_scalar(out=ynb[:], in0=ych[:], scalar1=self.rs[:, 2:3], scalar2=None, op0=ALU.mult), reads=["ych", "rs3"], writes=["ynb"])
                        self.transpose_into(ynb, "ynb", 16, ynT, "ynT", 0)
                        for half in range(2):
                            for c in range(16):
                                P.op("pe", lambda e, c=c, half=half: e.matmul(self.pA[:, half, :], ynT[:, c, :], Wout[:, c, half * 512:(half + 1) * 512], start=(c == 0), stop=(c == 15)), reads=["ynT", "Wout"], writes=["pA"])
                        self.tail(li, t, self.pA, "pA", xin, xout)
            P.barrier()

    def build(self):
        L, NT = self.L, self.NT
        nc = bass.Bass("TRN2", target_bir_lowering=False)
        self.nc = nc
        NL = len(self.layers)
        nssd = max(1, sum(1 for t in self.layers if t == "ssd"))
        natt = max(1, sum(1 for t in self.layers if t == "att"))
        x_d = self.dram_in("xs", [L, D])
        self.p_d = self.dram_in("ps", [NL, L, PLE])
        tm_d = self.dram_in("tmask", [128, NT])
        self.kext = self.dram_in("kext", [8, 5, L], BF16)
        self.qext = self.dram_in("qext", [3, 8, 5, L], BF16)
        self.dtab = self.dram_in("dtab", [8, 4, 128, 512])
        cst = self.dram_in("cst_f", [6, 128, 128])
        cstb = self.dram_in("cst_b", [4, 128, 128], BF16)
        selB_d = self.dram_in("selB_d", [32, 32, 128])
        shapes = dict(
            pre_norm_g=[NL, D], ssd_w_in=[nssd, D, 6208], ssd_conv_w=[nssd, 5, 4096], ssd_conv_b=[nssd, 4096],
            ssd_dt_bias=[nssd, 64], ssd_a_log=[nssd, 64], ssd_d_skip=[nssd, 32], ssd_norm_g=[nssd, 2048],
            ssd_w_out=[nssd, 2048, D], att_w_in=[natt, D, 4096], att_q_norm_g=[natt, 64], att_k_norm_g=[natt, 64],
            att_lam_q1=[natt, 64], att_lam_k1=[natt, 64], att_lam_q2=[natt, 64], att_lam_k2=[natt, 64],
            att_sub_norm_g=[natt, 128], att_w_out=[natt, D, D], ple_w_proj=[NL, PLE, D], ple_norm_g=[NL, D],
            ple_gate_norm_g=[NL, D], ple_w_gate=[NL, D, D])
        self.w = {k: self.dram_in(k, s) for k, s in shapes.items()}
        y_d = nc.dram_tensor("y", [L, D], F32, kind="ExternalOutput").ap()
        xres = self.scratch("xres", [L, D], F32)
        self.qT_d = self.scratch("qT_d", [8, 128, L], BF16)
        self.kT_d = self.scratch("kT_d", [8, 128, L], BF16)
        self.v_d = self.scratch("v_d", [L, D], BF16)
        self.sg_d = self.scratch("sg_d", [D, L], BF16)
        self.og_d = self.scratch("og_d", [D, L], BF16)
        if "ssd" in self.layers:
            self.xbc_d = self.scratch("xbc_d", [4096, L], F32)
            self.z_d = self.scratch("z_d", [L, 2048], F32)
            self.dt_d = self.scratch("dt_d", [L, 64], F32)
            self.yf_d = self.scratch("yf_d", [L, 2048], F32)
        with ExitStack() as es:
            block = es.enter_context(nc.Block())
            P = Prog(nc, es, block)
            self.P = P
            g = lambda n, s, d: self.sb(es, n, s, d)
            ps = lambda n, s, d: es.enter_context(nc.psum_tensor(n, s, d))
            self.pA = ps("pA", [128, 2, 512], F32)
            self.pB = ps("pB", [128, 2, 512], F32)
            self.pC = ps("pC", [128, 2, 512], F32)
            self.pT = ps("pT", [128, 8, 128], BF16)
            self.pS = ps("pS", [128, 512], F32)
            self.wst = [g(f"wst{i}", [128, 1024], F32) for i in range(2)]
            self.wcnt = 0
            self.ident = g("ident", [128, 128], BF16)
            self.negI = g("negI", [128, 128], BF16)
            self.maskF = g("maskF", [128, 4, 128], BF16)
            self.maskB = g("maskB", [128, 4, 128], BF16)
            self.ident_f = g("ident_f", [128, 128], F32)
            self.blk1 = g("blk1", [128, 128], F32)
            self.ones_f = g("ones_f", [128, 128], F32)
            self.U_f = g("U_f", [128, 128], F32)
            self.Lo_f = g("Lo_f", [128, 128], F32)
            self.epsc = g("epsc", [128, 1], F32)
            self.tm = g("tm", [128, NT], F32)
            self.gcols = g("gcols", [128, 8, 8], F32)
            self.gple = g("gple", [128, D], F32)
            self.xt = g("xt", [128, D], F32)
            self.xm = g("xm", [128, D], F32)
            self.junk = g("junk", [128, 1024], F32)
            self.gs = g("gs", [128, D], F32)
            self.ev = g("ev", [128, D], F32)
            self.hb = g("hb", [128, D], BF16)
            self.h2T = g("h2T", [128, 8, 128], BF16)
            self.pt = g("pt", [128, PLE], F32)
            self.pb = g("pb", [128, PLE], BF16)
            self.ppT = g("ppT", [128, 2, 128], BF16)
            self.ss = g("ss", [128, 4], F32)
            self.rs = g("rs", [128, 4], F32)
            P.op("pool", lambda e: e.memset(self.epsc[:], EPS), writes=["epsc"])
            P.dma("sp", self.ident[:], cstb[0], writes=["ident"])
            P.dma("sp", self.negI[:], cstb[1], writes=["negI"])
            for r in range(4):
                P.dma("sp", self.maskF[:, r, :], cstb[2], writes=["maskF"])
                P.dma("sp", self.maskB[:, r, :], cstb[3], writes=["maskB"])
            for i, tl in enumerate([self.ident_f, self.blk1, self.ones_f, self.U_f, self.Lo_f]):
                P.dma("pool", tl[:], cst[i], writes=[["ident_f", "blk1", "ones_f", "U_f", "Lo_f"][i]])
            P.dma("pool", self.tm[:], tm_d, writes=["tm"])
            for li in range(NL):
                P.dma("sp", self.gcols[:, li, :], self.w["pre_norm_g"][li].rearrange("(c p) -> p c", p=128), writes=["gcols"], allow_slow_non_contiguous=True)
                P.dma("sp", self.gcols[:, 4 + li, :], self.w["ple_gate_norm_g"][li].rearrange("(c p) -> p c", p=128), writes=["gcols"], allow_slow_non_contiguous=True)
            js = {"ssd": 0, "att": 0}
            for li, typ in enumerate(self.layers):
                xin = x_d if li == 0 else xres
                xout = y_d if li == NL - 1 else xres
                j = js[typ]
                js[typ] += 1
                if typ == "att":
                    self.att_layer(li, j, xin, xout)
                else:
                    self.ssd_layer(li, j, xin, xout)
            P.finish()
            self.ninstr = P.ninstr
        return nc


def host_consts(L, lens):
    bf = ml_dtypes.bfloat16
    slopes = np.array([2.0 ** (-8.0 * (i + 1) / 8) for i in range(8)], dtype=np.float64)
    pos = np.arange(L)
    hi = (pos // 128).astype(np.float64) * 128
    lo = (pos % 128).astype(np.float64)
    kext = np.zeros((8, 5, L), np.float64)
    qext = np.zeros((3, 8, 5, L), np.float64)
    for h in range(8):
        s = slopes[h]
        kext[h, 0] = s * hi
        kext[h, 1] = s * lo
        kext[h, 2] = 1
        kext[h, 3] = 1
        kext[h, 4] = np.where(pos < lens, 0.0, NEG)
        qext[0, h] = np.stack([np.ones(L), np.ones(L), -s * hi, -s * lo, np.ones(L)])
        qext[1, h] = np.stack([-np.ones(L), -np.ones(L), s * hi, s * lo, np.ones(L)])
        qext[2, h] = np.stack([np.zeros(L), np.zeros(L), np.zeros(L), np.zeros(L), np.ones(L)])
    kr = np.arange(128)[:, None]
    qr = np.arange(512)[None, :]
    dtab = np.zeros((8, 4, 128, 512), np.float32)
    for h in range(8):
        for jj in range(4):
            dtab[h, jj] = -slopes[h] * np.abs(qr - kr - 128 * jj)
    i128 = np.eye(128)
    blk1 = np.kron(np.eye(2), np.ones((64, 64)))
    k_ = np.arange(128)[:, None]
    l_ = np.arange(128)[None, :]
    U = (k_ <= l_).astype(np.float64)
    Lo = (k_ >= l_).astype(np.float64)
    cst_f = np.stack([i128, blk1, np.ones((128, 128)), U, Lo, np.zeros((128, 128))]).astype(np.float32)
    cst_b = np.stack([i128, NEG * i128, (l_ < k_).astype(np.float64), (l_ > k_).astype(np.float64)]).astype(bf)
    selB = np.zeros((32, 32, 128), np.float32)
    for r in range(32):
        selB[r, r, :] = 1
    tmask = np.ascontiguousarray((pos < lens).astype(np.float32).reshape(L // 128, 128).T)
    return dict(kext=kext.astype(bf), qext=qext.astype(bf), dtab=dtab, cst_f=cst_f, cst_b=cst_b, selB_d=selB, tmask=tmask)


_CACHE = {}


def run_trunk(xs_list, ps_list, lens_list, weights, layers, L):
    NL = len(layers)
    lam_inits = [0.8 - 0.6 * math.exp(-0.3 * i) for i in range(NL)]
    key = (L, tuple(layers))
    if key not in _CACHE:
        kb = KB(L, layers, lam_inits)
        _CACHE[key] = (kb, kb.build())
    kb, nc = _CACHE[key]
    wmap = {}
    for k, v in weights.items():
        a = np.ascontiguousarray(np.asarray(v, dtype=np.float32))
        if k in ("ssd_dt_bias", "ssd_a_log"):
            a = a.reshape(a.shape[0], 64)
        wmap[k] = a
    in_maps = []
    for c in range(8):
        m = dict(wmap)
        m.update(host_consts(L, lens_list[c]))
        m["xs"] = xs_list[c]
        m["ps"] = ps_list[c]
        in_maps.append(m)
    res = run_bass_kernel_spmd(nc, in_maps, core_ids=list(range(8)))
    return [r["y"] for r in res.results]


LAYERS = ["ssd", "att", "ssd", "att"]
WNAMES = ["pre_norm_g", "ssd_w_in", "ssd_conv_w", "ssd_conv_b", "ssd_dt_bias", "ssd_a_log", "ssd_d_skip", "ssd_norm_g",
          "ssd_w_out", "att_w_in", "att_q_norm_g", "att_k_norm_g", "att_lam_q1", "att_lam_k1", "att_lam_q2", "att_lam_k2",
          "att_sub_norm_g", "att_w_out", "ple_w_proj", "ple_norm_g", "ple_gate_norm_g", "ple_w_gate"]


def kernel(x_prompt, x_sample, p_prompt, p_sample, **weights):
    L = 16384
    xp = np.asarray(x_prompt, dtype=np.float32)
    xsm = np.asarray(x_sample, dtype=np.float32)
    pp = np.asarray(p_prompt, dtype=np.float32)
    psm = np.asarray(p_sample, dtype=np.float32)
    B, S = xp.shape[0], xp.shape[1]
    NL = pp.shape[0]
    xs_list, ps_list, lens = [], [], []
    for c in range(8):
        xs = np.zeros((L, D), np.float32)
        ps = np.zeros((NL, L, PLE), np.float32)
        if c < B:
            xs[:S] = xp[c]
            ps[:, :S] = pp[:, c]
            lens.append(S)
        elif c == B:
            xs[:] = xsm[0]
            ps[:] = psm[:, 0]
            lens.append(L)
        else:
            lens.append(L)
        xs_list.append(xs)
        ps_list.append(ps)
    ys = run_trunk(xs_list, ps_list, lens, weights, LAYERS, L)
    y_prompt = np.stack([ys[c][:S] for c in range(B)]).astype(np.float32)
    y_sample = ys[B][None].astype(np.float32)
    return (y_prompt, y_sample)
```

```python
import math
import numpy as np
import ml_dtypes
from contextlib import ExitStack
import concourse.bass as bass
import concourse.mybir as mybir
from concourse.bass_utils import run_bass_kernel_spmd

F32 = mybir.dt.float32
BF16 = mybir.dt.bfloat16
AF = mybir.ActivationFunctionType
ALU = mybir.AluOpType

NSLOT = 6
D = 1024
PLE = 256
EPS = 1e-6
NEG = -30000.0
class Prog:
    ENG = ["pe", "act", "dve", "pool", "sp"]

    def __init__(self, nc, es, block=None):
        self.nc = nc
        self.block = block
        if block is not None:
            self.handles = {"pe": block.tensor, "act": block.scalar, "dve": block.vector,
                            "pool": block.gpsimd, "sp": block.sync}
        self.items = {e: [] for e in self.ENG}
        self.count = {e: 0 for e in self.ENG}
        self.known = {e: {} for e in self.ENG}
        self.sem = {}
        for e in self.ENG:
            self.sem[("e", e)] = es.enter_context(nc.semaphore("s_" + e))
        self.dmaq = ["sp", "pool", "act"]
        self.dcount = {q: 0 for q in self.dmaq}
        for q in self.dmaq:
            for s in range(NSLOT):
                self.sem[("d", q, s)] = es.enter_context(nc.semaphore(f"d_{q}{s}"))
        self.lastw = {}
        self.readers = {}
        self.ninstr = 0

    def _deps(self, reads, writes):
        deps = {}

        def add(d):
            if d is None:
                return
            k, v = d
            if deps.get(k, 0) < v:
                deps[k] = v
        for r in reads:
            add(self.lastw.get(r))
        for w in writes:
            add(self.lastw.get(w))
            for d in self.readers.get(w, {}).items():
                add(d)
        return deps

    def _emit(self, eng, deps, fn, semkey, inc):
        waits = []
        kn = self.known[eng]
        for k, v in deps.items():
            if k == ("e", "pe") and eng == "pe":
                continue
            if kn.get(k, 0) >= v:
                continue
            kn[k] = v
            waits.append((k, v))
        self._push(eng, (waits, fn, semkey, inc))
        self.ninstr += 1 + len(waits)

    def _push(self, eng, item):
        if self.block is None:
            self.items[eng].append(item)
            return
        waits, fn, semkey, inc = item
        sem = self.sem

        def body(h):
            if fn is None:
                for k, v in waits:
                    h.wait_ge(sem[k], v)
                return
            NA = 1
            for k, v in waits[:-NA]:
                h.wait_ge(sem[k], v)
            ins = fn(h)
            for k, v in waits[-NA:]:
                ins._wait_ge(sem[k], v)
            ins.then_inc(sem[semkey], inc)
        self.handles[eng](body)

    def _mark(self, reads, writes, tag):
        for w in writes:
            self.lastw[w] = tag
            self.readers[w] = {}
        for r in reads:
            d = self.readers.setdefault(r, {})
            if d.get(tag[0], 0) < tag[1]:
                d[tag[0]] = tag[1]

    def op(self, eng, fn, reads=(), writes=()):
        deps = self._deps(reads, writes)
        self.count[eng] += 1
        tag = (("e", eng), self.count[eng])
        self._emit(eng, deps, fn, ("e", eng), 1)
        self._mark(reads, writes, tag)

    def dma(self, q, out, in_, reads=(), writes=(), **kw):
        deps = self._deps(reads, writes)
        j = self.dcount[q]
        self.dcount[q] += 1
        slot = j % NSLOT
        key = ("d", q, slot)
        if j >= NSLOT:
            v = 16 * (j // NSLOT)
            if deps.get(key, 0) < v:
                deps[key] = v
        tag = (key, 16 * (j // NSLOT + 1))
        self._emit(q, deps, lambda e: e.dma_start(out=out, in_=in_, **kw), key, 16)
        self._mark(reads, writes, tag)

    def barrier(self):
        deps = {}
        for e in self.ENG:
            if self.count[e]:
                deps[("e", e)] = self.count[e]
        for q in self.dmaq:
            j = self.dcount[q]
            for s in range(NSLOT):
                n = (j - s + NSLOT - 1) // NSLOT if j > s else 0
                if n:
                    deps[("d", q, s)] = 16 * n
        for e in self.ENG:
            waits = []
            kn = self.known[e]
            for k, v in deps.items():
                if k == ("e", "pe") and e == "pe":
                    continue
                if kn.get(k, 0) >= v:
                    continue
                kn[k] = v
                waits.append((k, v))
            if waits:
                self._push(e, (waits, None, None, 0))
                self.ninstr += len(waits)
        self.lastw = {}
        self.readers = {}

    def finish(self):
        self.barrier()

    def replay(self, block):
        nc = self.nc
        handles = {"pe": block.tensor, "act": block.scalar, "dve": block.vector,
                   "pool": block.gpsimd, "sp": block.sync}
        for e in self.ENG:
            items = self.items[e]
            sem = self.sem

            def body(h, items=items):
                for waits, fn, semkey, inc in items:
                    for k, v in waits:
                        h.wait_ge(sem[k], v)
                    if fn is not None:
                        fn(h).then_inc(sem[semkey], inc)
            handles[e](body)


class KB:
    def __init__(self, L, layers, lam_inits):
        self.L = L
        self.layers = layers
        self.lam_inits = lam_inits
        self.NT = L // 128

    def sb(self, st, name, shape, dt):
        self.uid = getattr(self, "uid", 0) + 1
        return st.enter_context(self.nc.sbuf_tensor(f"{name}_{self.uid}", shape, dt))

    def dram_in(self, name, shape, dt=F32):
        return self.nc.dram_tensor(name, list(shape), dt, kind="ExternalInput").ap()

    def scratch(self, name, shape, dt):
        return self.nc.dram_tensor(name, list(shape), dt, kind="Internal").ap()

    def load_w(self, st, name, src, K, N, scale=None):
        P = self.P
        KC = K // 128
        dst = self.sb(st, name, [128, KC, N], BF16)
        srcv = src.rearrange("(c p) n -> p c n", p=128)
        i = 0
        for c in range(KC):
            for n0 in range(0, N, 1024):
                n1 = min(N, n0 + 1024)
                b = self.wcnt % 2
                self.wcnt += 1
                stg = self.wst[b]
                P.dma("sp", stg[:, 0:n1 - n0], srcv[:, c, n0:n1], writes=[f"wst{b}"])
                eng = "pool" if b else "dve"
                if scale is None:
                    P.op(eng, lambda e, stg=stg, c=c, n0=n0, n1=n1: e.tensor_copy(out=dst[:, c, n0:n1], in_=stg[:, 0:n1 - n0]),
                         reads=[f"wst{b}"], writes=[name])
                else:
                    sc, sk = scale
                    P.op(eng, lambda e, stg=stg, c=c, n0=n0, n1=n1: e.tensor_scalar(out=dst[:, c, n0:n1], in0=stg[:, 0:n1 - n0], scalar1=sc[:, c:c + 1], scalar2=None, op0=ALU.mult),
                         reads=[f"wst{b}", sk], writes=[name])
        return dst

    def rstd_from_ss(self, ss_ap, ss_key, out_ap, out_key, n):
        P = self.P
        P.op("act", lambda e: e.activation(out=out_ap, in_=ss_ap, func=AF.Sqrt, scale=1.0 / n, bias=self.epsc[:, 0:1]),
             reads=[ss_key, "epsc"], writes=[out_key])
        P.op("dve", lambda e: e.reciprocal(out=out_ap, in_=out_ap), reads=[out_key], writes=[out_key])

    def norm_transpose(self, xt, xkey, dst, dkey, col0, mask_col=None):
        P = self.P
        P.op("pool", lambda e: e.memset(self.ss[:, 0:1], 0.0), writes=["ss"])
        P.op("dve", lambda e: e.scalar_tensor_tensor(out=self.junk[:, 0:D], in0=xt, scalar=1.0, in1=xt, op0=ALU.mult, op1=ALU.mult, accum_out=self.ss[:, 0:1]),
             reads=[xkey, "ss"], writes=["junk", "ss"])
        self.rstd_from_ss(self.ss[:, 0:1], "ss", self.rs[:, 0:1], "rs", D)
        if mask_col is not None:
            P.op("dve", lambda e: e.tensor_tensor(out=self.rs[:, 0:1], in0=self.rs[:, 0:1], in1=mask_col, op=ALU.mult), reads=["rs", "tm"], writes=["rs"])
        P.op("dve", lambda e: e.tensor_scalar(out=self.hb[:], in0=xt, scalar1=self.rs[:, 0:1], scalar2=None, op0=ALU.mult),
             reads=[xkey, "rs"], writes=["hb"])
        self.transpose_into(self.hb, "hb", 8, dst, dkey, col0)

    def transpose_into(self, src, skey, nchunk, dst, dkey, col0):
        P = self.P
        for c0 in range(0, nchunk, 8):
            n = min(8, nchunk - c0)
            for c in range(n):
                P.op("pe", lambda e, c=c: e.transpose(out=self.pT[:, c, :], in_=src[:, (c0 + c) * 128:(c0 + c + 1) * 128], identity=self.ident[:]),
                     reads=[skey, "ident"], writes=["pT"])
            P.op("act", lambda e, n=n, c0=c0: e.copy(out=dst[:, c0:c0 + n, col0:col0 + 128], in_=self.pT[:, 0:n, :]), reads=["pT"], writes=[dkey])

    def tail(self, li, t, pmix, pmix_key, xin, xout):
        P = self.P
        r0 = t * 128
        P.dma("pool", self.xt[:], xin[r0:r0 + 128, :], reads=[("x", t)], writes=["xt"])
        P.dma("pool", self.pt[:], self.p_d[li, r0:r0 + 128, :], writes=["pt"])
        P.op("dve", lambda e: e.tensor_tensor(out=self.xm[:], in0=pmix.rearrange("p a b -> p (a b)"), in1=self.xt[:], op=ALU.add),
             reads=[pmix_key, "xt"], writes=["xm"])
        self.norm_transpose(self.xm[:], "xm", self.h2T, "h2T", 0)
        P.op("pool", lambda e: e.tensor_copy(out=self.pb[:], in_=self.pt[:]), reads=["pt"], writes=["pb"])
        self.transpose_into(self.pb, "pb", 2, self.ppT, "ppT", 0)
        for half in range(2):
            for c in range(8):
                P.op("pe", lambda e, c=c, half=half: e.matmul(self.pB[:, half, :], self.h2T[:, c, :], self.Wg[:, c, half * 512:(half + 1) * 512], start=(c == 0), stop=(c == 7)),
                     reads=["h2T", "Wg"], writes=["pB"])
        for half in range(2):
            for c in range(2):
                P.op("pe", lambda e, c=c, half=half: e.matmul(self.pC[:, half, :], self.ppT[:, c, :], self.Wp[:, c, half * 512:(half + 1) * 512], start=(c == 0), stop=(c == 1)),
                     reads=["ppT", "Wp"], writes=["pC"])
        P.op("act", lambda e: e.activation(out=self.gs[:], in_=self.pB.rearrange("p a b -> p (a b)"), func=AF.Sigmoid), reads=["pB"], writes=["gs"])
        P.op("pool", lambda e: e.memset(self.ss[:, 1:2], 0.0), writes=["ss2"])
        P.op("act", lambda e: e.activation(out=self.junk[:, 0:D], in_=self.pC.rearrange("p a b -> p (a b)"), func=AF.Square, accum_out=self.ss[:, 1:2]),
             reads=["pC", "ss2"], writes=["junk", "ss2"])
        self.rstd_from_ss(self.ss[:, 1:2], "ss2", self.rs[:, 1:2], "rs2", D)
        P.op("dve", lambda e: e.scalar_tensor_tensor(out=self.ev[:], in0=self.pC.rearrange("p a b -> p (a b)"), scalar=self.rs[:, 1:2], in1=self.gple[:], op0=ALU.mult, op1=ALU.mult),
             reads=["pC", "rs2", "gple"], writes=["ev"])
        P.op("pool", lambda e: e.tensor_tensor(out=self.ev[:], in0=self.ev[:], in1=self.gs[:], op=ALU.mult), reads=["ev", "gs"], writes=["ev"])
        P.op("dve", lambda e: e.tensor_tensor(out=self.ev[:], in0=self.ev[:], in1=self.xm[:], op=ALU.add), reads=["ev", "xm"], writes=["ev"])
        P.dma("sp", xout[r0:r0 + 128, :], self.ev[:], reads=["ev"], writes=[("x", t)])

    def load_tail_weights(self, st, li):
        self.Wg = self.load_w(st, "Wg", self.w["ple_w_gate"][li], D, D, scale=(self.gcols[:, 4 + li, :], "gcols"))
        self.Wp = self.load_w(st, "Wp", self.w["ple_w_proj"][li], PLE, D)
        self.P.dma("pool", self.gple[:], self.w["ple_norm_g"][li].partition_broadcast(128), reads=[], writes=["gple"])

    def att_layer(self, li, j, xin, xout):
        nc, P, L, NT = self.nc, self.P, self.L, self.NT
        w = self.w
        lam_init = self.lam_inits[li]
        NQ = L // 512
        with ExitStack() as st:
            Win = self.load_w(st, "Win", w["att_w_in"][j], D, 4096, scale=(self.gcols[:, li, :], "gcols"))
            hT = self.sb(st, "hT", [128, 8, 512], BF16)
            sq = self.sb(st, "sq", [128, 512], F32)
            rr = self.sb(st, "rr", [128, 512], F32)
            qn = [self.sb(st, f"qn{i}", [128, 512], BF16) for i in range(2)]
            vb = [self.sb(st, f"vb{i}", [128, 1024], BF16) for i in range(2)]
            gqk = self.sb(st, "gqk", [128, 2], F32)
            for m in range(2):
                P.dma("sp", gqk[m * 64:(m + 1) * 64, 0:1], w["att_q_norm_g"][j].rearrange("(d o) -> d o", o=1), writes=["gqk"], allow_slow_non_contiguous=True)
                P.dma("sp", gqk[m * 64:(m + 1) * 64, 1:2], w["att_k_norm_g"][j].rearrange("(d o) -> d o", o=1), writes=["gqk"], allow_slow_non_contiguous=True)
            P.op("dve", lambda e: e.tensor_scalar(out=gqk[:, 0:1], in0=gqk[:, 0:1], scalar1=0.125, scalar2=None, op0=ALU.mult), reads=["gqk"], writes=["gqk"])
            banks = [(self.pA, "pA", 0), (self.pA, "pA", 1), (self.pB, "pB", 0), (self.pB, "pB", 1)]
            bi = 0
            for tt in range(NQ):
                for sub in range(4):
                    t = tt * 4 + sub
                    P.dma("pool", self.xt[:], xin[t * 128:(t + 1) * 128, :], reads=[("x", t)], writes=["xt"])
                    self.norm_transpose(self.xt[:], "xt", hT, "hT", sub * 128, mask_col=self.tm[:, t:t + 1])
                c0 = tt * 512
                for cg in range(16):
                    pt_, pk, ph = banks[bi % 4]
                    pk = pk + str(ph)
                    bi += 1
                    for c in range(8):
                        P.op("pe", lambda e, c=c, cg=cg, pt_=pt_, ph=ph: e.matmul(pt_[:, ph, :], Win[:, c, cg * 128:(cg + 1) * 128], hT[:, c, :], start=(c == 0), stop=(c == 7)),
                             reads=["Win", "hT"], writes=[pk])
                    P.op("act", lambda e, pt_=pt_, ph=ph: e.activation(out=sq[:], in_=pt_[:, ph, :], func=AF.Square), reads=[pk], writes=["sq"])
                    P.op("pe", lambda e: e.matmul(self.pC[:, 0, :], self.blk1[:], sq[:], start=True, stop=True), reads=["sq", "blk1"], writes=["pC0"])
                    P.op("act", lambda e: e.activation(out=rr[:], in_=self.pC[:, 0, :], func=AF.Sqrt, scale=1.0 / 64, bias=self.epsc[:, 0:1]), reads=["pC0", "epsc"], writes=["rr"])
                    P.op("dve", lambda e: e.reciprocal(out=rr[:], in_=rr[:]), reads=["rr"], writes=["rr"])
                    qb = qn[cg % 2]
                    gc = gqk[:, 0:1] if cg < 8 else gqk[:, 1:2]
                    P.op("dve", lambda e, pt_=pt_, ph=ph, qb=qb, gc=gc: e.scalar_tensor_tensor(out=qb[:], in0=pt_[:, ph, :], scalar=gc, in1=rr[:], op0=ALU.mult, op1=ALU.mult),
                         reads=[pk, "gqk", "rr"], writes=[f"qn{cg % 2}"])
                    dst = self.qT_d if cg < 8 else self.kT_d
                    P.dma("sp", dst[cg % 8, :, c0:c0 + 512], qb[:], reads=[f"qn{cg % 2}"], writes=["qkT_d"])
                for sub in range(4):
                    t = tt * 4 + sub
                    for half in range(2):
                        pt_, pk, ph = banks[bi % 4]
                        pk = pk + str(ph)
                        bi += 1
                        for c in range(8):
                            P.op("pe", lambda e, c=c, half=half, sub=sub, pt_=pt_, ph=ph: e.matmul(pt_[:, ph, :], hT[:, c, sub * 128:(sub + 1) * 128], Win[:, c, 2048 + half * 512:2048 + (half + 1) * 512], start=(c == 0), stop=(c == 7)),
                                 reads=["Win", "hT"], writes=[pk])
                        P.op("act", lambda e, pt_=pt_, ph=ph, half=half, t=t: e.copy(out=vb[t % 2][:, half * 512:(half + 1) * 512], in_=pt_[:, ph, :]), reads=[pk], writes=[f"vb{t % 2}"])
                    P.dma("sp", self.v_d[t * 128:(t + 1) * 128, :], vb[t % 2][:], reads=[f"vb{t % 2}"], writes=["v_d"])
                for cg in range(8):
                    pt_, pk, ph = banks[bi % 4]
                    pk = pk + str(ph)
                    bi += 1
                    for c in range(8):
                        P.op("pe", lambda e, c=c, cg=cg, pt_=pt_, ph=ph: e.matmul(pt_[:, ph, :], Win[:, c, 3072 + cg * 128:3072 + (cg + 1) * 128], hT[:, c, :], start=(c == 0), stop=(c == 7)),
                             reads=["Win", "hT"], writes=[pk])
                    qb = qn[cg % 2]
                    P.op("act", lambda e, pt_=pt_, ph=ph, qb=qb: e.activation(out=qb[:], in_=pt_[:, ph, :], func=AF.Silu), reads=[pk], writes=[f"qn{cg % 2}"])
                    P.dma("sp", self.sg_d[cg * 128:(cg + 1) * 128, c0:c0 + 512], qb[:], reads=[f"qn{cg % 2}"], writes=["sg_d"])
            P.barrier()
        with ExitStack() as st:
            K = [self.sb(st, f"K{m}", [69, L], BF16) for m in range(2)]
            V = self.sb(st, "V", [128, NT, 128], BF16)
            Q = {(v, m, b): self.sb(st, f"Q{v}{m}{b}", [69, 512], BF16) for v in "ABC" for m in range(2) for b in range(2)}
            sgt = [self.sb(st, f"sgt{b}", [128, 512], BF16) for b in range(2)]
            Pt = [self.sb(st, f"Pt{b}", [128, 4, 512], BF16) for b in range(2)]
            Sd = self.sb(st, "Sd", [128, 2, 512], F32)
            acc = self.sb(st, "acc", [128, 4, 512], F32)
            rd = self.sb(st, "rd", [128, 2, 512], F32)
            o1 = self.sb(st, "o1", [128, 512], F32)
            t2 = self.sb(st, "t2", [128, 512], F32)
            osq = self.sb(st, "osq", [128, 512], F32)
            r2 = self.sb(st, "r2", [128, 512], F32)
            ogb = [self.sb(st, f"ogb{b}", [128, 512], BF16) for b in range(2)]
            dt4 = self.sb(st, "dt4", [128, 4, 512], F32)
            lamv = self.sb(st, "lamv", [128, 4, 64], F32)
            lams = self.sb(st, "lams", [128, 4], F32)
            gsub = self.sb(st, "gsub", [128, 1], F32)
            for i, nm in enumerate(["att_lam_q1", "att_lam_k1", "att_lam_q2", "att_lam_k2"]):
                P.dma("sp", lamv[:, i, :], w[nm][j].partition_broadcast(128), writes=["lamv"])
            P.op("pool", lambda e: e.memset(lams[:], 0.0), writes=["lams"])
            P.op("dve", lambda e: e.scalar_tensor_tensor(out=self.junk[:, 0:64], in0=lamv[:, 0, :], scalar=1.0, in1=lamv[:, 1, :], op0=ALU.mult, op1=ALU.mult, accum_out=lams[:, 0:1]),
                 reads=["lamv", "lams"], writes=["junk", "lams"])
            P.op("dve", lambda e: e.scalar_tensor_tensor(out=self.junk[:, 0:64], in0=lamv[:, 2, :], scalar=1.0, in1=lamv[:, 3, :], op0=ALU.mult, op1=ALU.mult, accum_out=lams[:, 1:2]),
                 reads=["lamv", "lams", "junk"], writes=["junk", "lams"])
            P.op("act", lambda e: e.activation(out=lams[:, 0:2], in_=lams[:, 0:2], func=AF.Exp), reads=["lams"], writes=["lams"])
            P.op("dve", lambda e: e.scalar_tensor_tensor(out=lams[:, 2:3], in0=lams[:, 1:2], scalar=-lam_init, in1=lams[:, 0:1], op0=ALU.add, op1=ALU.subtract),
                 reads=["lams"], writes=["lams"])
            P.dma("sp", gsub[:], w["att_sub_norm_g"][j].rearrange("(d o) -> d o", o=1), writes=["gsub"], allow_slow_non_contiguous=True)
            P.op("dve", lambda e: e.tensor_scalar(out=gsub[:], in0=gsub[:], scalar1=1.0 - lam_init, scalar2=None, op0=ALU.mult), reads=["gsub"], writes=["gsub"])
            nKT = L // 128
            for h in range(8):
                for m in range(2):
                    P.dma("sp", K[m][0:64, :], self.kT_d[h, m * 64:(m + 1) * 64, :], reads=["qkT_d"], writes=[f"K{m}"])
                    P.dma("pool", K[m][64:69, :], self.kext[h], writes=[f"K{m}"])
                P.dma("sp", V[:], self.v_d[:, h * 128:(h + 1) * 128].rearrange("(t p) e -> p t e", p=128), reads=["v_d"], writes=["V"])
                P.dma("pool", dt4[:], self.dtab[h].rearrange("j p q -> p j q"), writes=["dt4"])
                for qt in range(NQ):
                    b = qt % 2
                    q0 = qt * 512
                    for m in range(2):
                        for vi, v in enumerate("ABC"):
                            P.dma("pool" if m else "sp", Q[(v, m, b)][0:64, :], self.qT_d[h, m * 64:(m + 1) * 64, q0:q0 + 512], reads=["qkT_d"], writes=[f"Q{v}{m}{b}"])
                            P.dma("pool" if m else "sp", Q[(v, m, b)][64:69, :], self.qext[vi, h, :, q0:q0 + 512], writes=[f"Q{v}{m}{b}"])
                    P.dma("pool", sgt[b][:], self.sg_d[h * 128:(h + 1) * 128, q0:q0 + 512], reads=["sg_d"], writes=[f"sgt{b}"])
                    items = []
                    kt = 0
                    while kt < nKT:
                        if (kt + 1) * 128 <= q0:
                            items.append(("A", kt, 2)); kt += 2
                        elif kt * 128 >= q0 + 512:
                            items.append(("B", kt, 2)); kt += 2
                        else:
                            items.append(("C", kt, 1)); kt += 1
                    P.op("pool", lambda e: e.memset(acc[:].rearrange("p a q -> p (a q)"), 0.0), reads=[], writes=["acc"])

                    def emit_qk(it):
                        v, kt, n = it
                        for u in range(n):
                            S, sk = (self.pA, "pA") if u == 0 else (self.pB, "pB")
                            for m in range(2):
                                P.op("pe", lambda e, m=m, S=S, kk=kt + u, v=v: e.matmul(S[:, m, :], K[m][:, kk * 128:(kk + 1) * 128], Q[(v, m, b)][:], start=True, stop=True),
                                     reads=[f"K{m}", f"Q{v}{m}{b}"], writes=[sk])

                    first = [True]
                    for idx, it in enumerate(items):
                        v, kt, n = it
                        pb_ = idx % 2
                        if idx == 0:
                            emit_qk(it)
                        if v == "C":
                            jj = kt - qt * 4
                            P.op("dve", lambda e, jj=jj: e.tensor_tensor(out=Sd[:], in0=self.pA[:], in1=dt4[:, jj, :].unsqueeze(1).to_broadcast([128, 2, 512]), op=ALU.add),
                                 reads=["pA", "dt4"], writes=["Sd"])
                            P.op("act", lambda e, pb_=pb_: e.activation(out=Pt[pb_][:, 0:2, :], in_=Sd[:], func=AF.Exp), reads=["Sd"], writes=[f"Pt{pb_}"])
                        else:
                            P.op("act", lambda e, pb_=pb_: e.activation(out=Pt[pb_][:], in_=self.pAB[:], func=AF.Exp), reads=["pA", "pB"], writes=[f"Pt{pb_}"])
                        if idx + 1 < len(items):
                            emit_qk(items[idx + 1])
                        for u in range(n):
                            for m in range(2):
                                st_ = first[0] and u == 0
                                sp_ = (idx == len(items) - 1) and (u == n - 1)
                                P.op("pe", lambda e, m=m, kk=kt + u, u=u, pb_=pb_, st_=st_, sp_=sp_: e.matmul(self.pC[:, m, :], V[:, kk, :], Pt[pb_][:, 2 * u + m, :], start=st_, stop=sp_),
                                     reads=["V", f"Pt{pb_}"], writes=["pC"])
                        first[0] = False
                        w_ = 2 * n
                        P.op("dve", lambda e, pb_=pb_, w_=w_: e.tensor_tensor(out=acc[:, 0:w_, :], in0=acc[:, 0:w_, :], in1=Pt[pb_][:, 0:w_, :], op=ALU.add), reads=[f"Pt{pb_}", "acc"], writes=["acc"])
                    for m in range(2):
                        for u in range(2):
                            P.op("pe", lambda e, m=m, u=u: e.matmul(self.pA[:, m, :], self.ones_f[:], acc[:, 2 * u + m, :], start=(u == 0), stop=(u == 1)), reads=["acc", "ones_f"], writes=["pA"])
                    P.op("dve", lambda e: e.tensor_scalar(out=rd[:], in0=self.pA[:], scalar1=1e-30, scalar2=None, op0=ALU.add), reads=["pA"], writes=["rd"])
                    P.op("dve", lambda e: e.reciprocal(out=rd[:], in_=rd[:]), reads=["rd"], writes=["rd"])
                    P.op("dve", lambda e: e.tensor_tensor(out=o1[:], in0=self.pC[:, 0, :], in1=rd[:, 0, :], op=ALU.mult), reads=["pC", "rd"], writes=["o1"])
                    P.op("dve", lambda e: e.tensor_tensor(out=t2[:], in0=self.pC[:, 1, :], in1=rd[:, 1, :], op=ALU.mult), reads=["pC", "rd"], writes=["t2"])
                    P.op("dve", lambda e: e.scalar_tensor_tensor(out=o1[:], in0=t2[:], scalar=lams[:, 2:3], in1=o1[:], op0=ALU.mult, op1=ALU.add), reads=["t2", "o1", "lams"], writes=["o1"])
                    P.op("pool", lambda e: e.tensor_tensor(out=osq[:], in0=o1[:], in1=o1[:], op=ALU.mult), reads=["o1"], writes=["osq"])
                    P.op("pe", lambda e: e.matmul(self.pB[:, 0, :], self.ones_f[:], osq[:], start=True, stop=True), reads=["osq", "ones_f"], writes=["pB"])
                    P.op("act", lambda e: e.activation(out=r2[:], in_=self.pB[:, 0, :], func=AF.Sqrt, scale=1.0 / 128, bias=self.epsc[:, 0:1]), reads=["pB", "epsc"], writes=["r2"])
                    P.op("dve", lambda e: e.reciprocal(out=r2[:], in_=r2[:]), reads=["r2"], writes=["r2"])
                    P.op("dve", lambda e: e.scalar_tensor_tensor(out=o1[:], in0=o1[:], scalar=gsub[:, 0:1], in1=r2[:], op0=ALU.mult, op1=ALU.mult), reads=["o1", "gsub", "r2"], writes=["o1"])
                    P.op("dve", lambda e, b=b: e.tensor_tensor(out=ogb[b][:], in0=o1[:], in1=sgt[b][:], op=ALU.mult), reads=["o1", f"sgt{b}"], writes=[f"ogb{b}"])
                    P.dma("sp", self.og_d[h * 128:(h + 1) * 128, q0:q0 + 512], ogb[b][:], reads=[f"ogb{b}"], writes=["og_d"])
            P.barrier()
        with ExitStack() as st:
            Wout = self.load_w(st, "Wout", w["att_w_out"][j], D, D)
            self.load_tail_weights(st, li)
            ogt = [self.sb(st, f"ogt{b}", [128, 8, 128], BF16) for b in range(2)]
            for t in range(NT):
                b = t % 2
                P.dma("sp", ogt[b][:], self.og_d[:, t * 128:(t + 1) * 128].rearrange("(h e) t -> e h t", e=128), reads=["og_d"], writes=[f"ogt{b}"])
                for half in range(2):
                    for h in range(8):
                        P.op("pe", lambda e, h=h, half=half, b=b: e.matmul(self.pA[:, half, :], ogt[b][:, h, :], Wout[:, h, half * 512:(half + 1) * 512], start=(h == 0), stop=(h == 7)),
                             reads=[f"ogt{b}", "Wout"], writes=["pA"])
                self.tail(li, t, self.pA, "pA", xin, xout)
            P.barrier()

    def ssd_layer(self, li, j, xin, xout):
        nc, P, L, NT = self.nc, self.P, self.L, self.NT
        w = self.w
        NQ = L // 512
        with ExitStack() as st:
            Win = self.load_w(st, "Win", w["ssd_w_in"][j], D, 6208, scale=(self.gcols[:, li, :], "gcols"))
            hT = self.sb(st, "hT", [128, 8, 512], BF16)
            rawo = [self.sb(st, f"rawo{i}", [128, 512], F32) for i in range(2)]
            zs = self.sb(st, "zs", [128, 2048], F32)
            dtb = self.sb(st, "dtb", [128, 64], F32)
            dts = self.sb(st, "dts", [128, 64], F32)
            onec = self.sb(st, "onec", [128, 1], F32)
            P.op("pool", lambda e: e.memset(onec[:], 1.0), writes=["onec"])
            P.dma("sp", dtb[:], w["ssd_dt_bias"][j].partition_broadcast(128), writes=["dtb"])
            banks = [(self.pA, "pA", 0), (self.pA, "pA", 1), (self.pB, "pB", 0), (self.pB, "pB", 1)]
            bi = 0
            for tt in range(NQ):
                c0 = tt * 512
                for sub in range(4):
                    t = tt * 4 + sub
                    P.dma("pool", self.xt[:], xin[t * 128:(t + 1) * 128, :], reads=[("x", t)], writes=["xt"])
                    self.norm_transpose(self.xt[:], "xt", hT, "hT", sub * 128, mask_col=self.tm[:, t:t + 1])
                for cg in range(32):
                    pt_, pk, ph = banks[bi % 4]
                    pk = pk + str(ph)
                    bi += 1
                    for c in range(8):
                        P.op("pe", lambda e, c=c, cg=cg, pt_=pt_, ph=ph: e.matmul(pt_[:, ph, :], Win[:, c, 2048 + cg * 128:2048 + (cg + 1) * 128], hT[:, c, :], start=(c == 0), stop=(c == 7)),
                             reads=["Win", "hT"], writes=[pk])
                    ro = rawo[cg % 2]
                    P.op("act", lambda e, pt_=pt_, ph=ph, ro=ro: e.copy(out=ro[:], in_=pt_[:, ph, :]), reads=[pk], writes=[f"rawo{cg % 2}"])
                    P.dma("sp", self.xbc_d[cg * 128:(cg + 1) * 128, c0:c0 + 512], ro[:], reads=[f"rawo{cg % 2}"], writes=["xbc_d"])
                for sub in range(4):
                    t = tt * 4 + sub
                    for qq in range(4):
                        pt_, pk, ph = banks[bi % 4]
                        pk = pk + str(ph)
                        bi += 1
                        for c in range(8):
                            P.op("pe", lambda e, c=c, qq=qq, sub=sub, pt_=pt_, ph=ph: e.matmul(pt_[:, ph, :], hT[:, c, sub * 128:(sub + 1) * 128], Win[:, c, qq * 512:(qq + 1) * 512], start=(c == 0), stop=(c == 7)),
                                 reads=["Win", "hT"], writes=[pk])
                        P.op("act", lambda e, pt_=pt_, ph=ph, qq=qq: e.activation(out=zs[:, qq * 512:(qq + 1) * 512], in_=pt_[:, ph, :], func=AF.Silu), reads=[pk], writes=["zs"])
                    P.dma("sp", self.z_d[t * 128:(t + 1) * 128, :], zs[:], reads=["zs"], writes=["z_d"])
                    for c in range(8):
                        P.op("pe", lambda e, c=c, sub=sub: e.matmul(self.pS[:, 0:64], hT[:, c, sub * 128:(sub + 1) * 128], Win[:, c, 6144:6208], start=(c == 0), stop=(c == 7)),
                             reads=["Win", "hT"], writes=["pS"])
                    P.op("dve", lambda e: e.tensor_tensor(out=dts[:], in0=self.pS[:, 0:64], in1=dtb[:], op=ALU.add), reads=["pS", "dtb"], writes=["dts"])
                    P.op("act", lambda e: e.activation(out=dts[:], in_=dts[:], func=AF.Exp), reads=["dts"], writes=["dts"])
                    P.op("act", lambda e: e.activation(out=dts[:], in_=dts[:], func=AF.Ln, bias=onec[:, 0:1]), reads=["dts", "onec"], writes=["dts"])
                    P.op("dve", lambda e, t=t: e.tensor_scalar(out=dts[:], in0=dts[:], scalar1=self.tm[:, t:t + 1], scalar2=None, op0=ALU.mult), reads=["dts", "tm"], writes=["dts"])
                    P.dma("sp", self.dt_d[t * 128:(t + 1) * 128, :], dts[:], reads=["dts"], writes=["dt_d"])
            P.barrier()
        with ExitStack() as st:
            Wout = self.load_w(st, "Wout", w["ssd_w_out"][j], 2048, D, scale=None)
            ng = self.sb(st, "ng", [128, 16], F32)
            P.dma("sp", ng[:], w["ssd_norm_g"][j].rearrange("(c p) -> p c", p=128), writes=["ng"], allow_slow_non_contiguous=True)
            for c in range(16):
                P.op("pool", lambda e, c=c: e.tensor_scalar(out=self.junk[:, 0:D], in0=Wout[:, c, :], scalar1=ng[:, c:c + 1], scalar2=None, op0=ALU.mult), reads=["Wout", "ng"], writes=["junk"])
                P.op("pool", lambda e, c=c: e.tensor_copy(out=Wout[:, c, :], in_=self.junk[:, 0:D]), reads=["junk"], writes=["Wout"])
            self.load_tail_weights(st, li)
            cw = self.sb(st, "cw", [128, 32, 5], F32)
            cb = self.sb(st, "cb", [128, 32], F32)
            for k in range(5):
                for q4 in range(4):
                    P.dma("sp", cw[:, q4 * 8:(q4 + 1) * 8, k:k + 1], w["ssd_conv_w"][j, k, q4 * 1024:(q4 + 1) * 1024].rearrange("(g p o) -> p g o", p=128, o=1), writes=["cw"], allow_slow_non_contiguous=True)
            for q4 in range(4):
                P.dma("sp", cb[:, q4 * 8:(q4 + 1) * 8], w["ssd_conv_b"][j, q4 * 1024:(q4 + 1) * 1024].rearrange("(g p) -> p g", p=128), writes=["cb"], allow_slow_non_contiguous=True)
            abc = self.sb(st, "abc", [128, 64], F32)
            Dbc = self.sb(st, "Dbc", [128, 32], F32)
            P.dma("sp", abc[:], w["ssd_a_log"][j].partition_broadcast(128), writes=["abc"])
            P.dma("sp", Dbc[:], w["ssd_d_skip"][j].partition_broadcast(128), writes=["Dbc"])
            P.op("act", lambda e: e.activation(out=abc[:], in_=abc[:], func=AF.Exp), reads=["abc"], writes=["abc"])
            P.op("dve", lambda e: e.tensor_scalar(out=abc[:], in0=abc[:], scalar1=-1.0, scalar2=None, op0=ALU.mult), reads=["abc"], writes=["abc"])
            raw = [self.sb(st, f"raw{i}", [128, 516], F32) for i in range(2)]
            cacc = [self.sb(st, f"cacc{i}", [128, 512], F32) for i in range(2)]
            XT = self.sb(st, "XT", [128, 16, 512], BF16)
            BT = self.sb(st, "BT", [128, 8, 512], BF16)
            CT = self.sb(st, "CT", [128, 8, 512], BF16)
            xtok = self.sb(st, "xtok", [128, 16, 128], BF16)
            btok = self.sb(st, "btok", [128, 8, 128], BF16)
            dtc = self.sb(st, "dtc", [128, 32], F32)
            dta = self.sb(st, "dta", [128, 32], F32)
            cs = self.sb(st, "cs", [128, 32], F32)
            Ee = self.sb(st, "Ee", [128, 32], F32)
            dd = self.sb(st, "dd", [128, 32], F32)
            dtot = self.sb(st, "dtot", [128, 32], F32)
            wdt = self.sb(st, "wdt", [128, 32], F32)
            csT = self.sb(st, "csT", [32, 128], F32)
            ncsT = self.sb(st, "ncsT", [32, 128], F32)
            xdt = self.sb(st, "xdt", [128, 32, 64], BF16)
            xdd = self.sb(st, "xdd", [128, 32, 64], BF16)
            cbs = self.sb(st, "cbs", [128, 128], F32)
            Lt = self.sb(st, "Lt", [128, 4, 128], F32)
            Mt = self.sb(st, "Mt", [128, 4, 128], BF16)
            tmpo = self.sb(st, "tmpo", [128, 4, 64], F32)
            ych = self.sb(st, "ych", [128, 2048], F32)
            yft = self.sb(st, "yft", [128, 2048], F32)
            ynb = self.sb(st, "ynb", [128, 2048], BF16)
            ynT = self.sb(st, "ynT", [128, 16, 128], BF16)
            Hf = self.sb(st, "Hf", [128, 8, 256], F32)
            Hb = self.sb(st, "Hb", [128, 8, 256], BF16)

            def prep_tile(tt):
                t0 = tt * 512
                lo = max(0, t0 - 2)
                hi = min(L, t0 + 514)
                for cg in range(32):
                    b = cg % 2
                    eng = "dve"
                    rw = raw[b]
                    if lo > t0 - 2:
                        P.op(eng, lambda e, rw=rw: e.memset(rw[:, 0:2], 0.0), writes=[f"raw{b}"])
                    if hi < t0 + 514:
                        P.op(eng, lambda e, rw=rw: e.memset(rw[:, 514:516], 0.0), writes=[f"raw{b}"])
                    P.dma("sp" if b == 0 else "act", rw[:, lo - (t0 - 2):hi - (t0 - 2)], self.xbc_d[cg * 128:(cg + 1) * 128, lo:hi], reads=["xbc_d"], writes=[f"raw{b}"])
                    ca = cacc[b]
                    P.op(eng, lambda e, rw=rw, ca=ca, cg=cg: e.tensor_scalar(out=ca[:], in0=rw[:, 0:512], scalar1=cw[:, cg, 0:1], scalar2=None, op0=ALU.mult),
                         reads=[f"raw{b}", "cw"], writes=[f"cacc{b}"])
                    for k in range(1, 5):
                        P.op(eng, lambda e, rw=rw, ca=ca, cg=cg, k=k: e.scalar_tensor_tensor(out=ca[:], in0=rw[:, k:k + 512], scalar=cw[:, cg, k:k + 1], in1=ca[:], op0=ALU.mult, op1=ALU.add),
                             reads=[f"raw{b}", "cw", f"cacc{b}"], writes=[f"cacc{b}"])
                    if cg < 16:
                        dst, dk = XT[:, cg, :], "XT"
                    elif cg < 24:
                        dst, dk = BT[:, cg - 16, :], "BT"
                    else:
                        dst, dk = CT[:, cg - 24, :], "CT"
                    P.op("act", lambda e, ca=ca, cg=cg, dst=dst: e.activation(out=dst, in_=ca[:], func=AF.Silu, bias=cb[:, cg:cg + 1]), reads=[f"cacc{b}", "cb"], writes=[dk])

            def chunk(t, l0, bwd):
                dcol = 32 if bwd else 0
                Tm = self.Lo_f if bwd else self.U_f
                tk = "Lo_f" if bwd else "U_f"
                mk, mkk = (self.maskB, "maskB") if bwd else (self.maskF, "maskF")
                for c0_ in (0, 8):
                    for c in range(8):
                        P.op("pe", lambda e, c=c, c0_=c0_: e.transpose(out=self.pT[:, c, :], in_=XT[:, c0_ + c, l0:l0 + 128], identity=self.ident[:]), reads=["XT", "ident"], writes=["pT"])
                    P.op("act", lambda e, c0_=c0_: e.copy(out=xtok[:, c0_:c0_ + 8, :], in_=self.pT[:]), reads=["pT"], writes=["xtok"])
                for c in range(8):
                    P.op("pe", lambda e, c=c: e.transpose(out=self.pT[:, c, :], in_=BT[:, c, l0:l0 + 128], identity=self.ident[:]), reads=["BT", "ident"], writes=["pT"])
                P.op("act", lambda e: e.copy(out=btok[:], in_=self.pT[:]), reads=["pT"], writes=["btok"])
                P.dma("pool", dtc[:], self.dt_d[t * 128:(t + 1) * 128, dcol:dcol + 32], reads=["dt_d"], writes=["dtc"])
                P.op("dve", lambda e: e.tensor_tensor(out=dta[:], in0=dtc[:], in1=abc[:, dcol:dcol + 32], op=ALU.mult), reads=["dtc", "abc"], writes=["dta"])
                P.op("pe", lambda e: e.matmul(self.pS[:, 0:32], Tm[:], dta[:], start=True, stop=True), reads=[tk, "dta"], writes=["pS"])
                P.op("pe", lambda e: e.matmul(self.pS[:, 32:64], self.ones_f[:], dta[:], start=True, stop=True), reads=["ones_f", "dta"], writes=["pS"])
                P.op("act", lambda e: e.copy(out=cs[:], in_=self.pS[:, 0:32]), reads=["pS"], writes=["cs"])
                P.op("act", lambda e: e.activation(out=Ee[:], in_=self.pS[:, 0:32], func=AF.Exp), reads=["pS"], writes=["Ee"])
                P.op("act", lambda e: e.activation(out=dtot[:], in_=self.pS[:, 32:64], func=AF.Exp), reads=["pS"], writes=["dtot"])
                P.op("dve", lambda e: e.tensor_tensor(out=dd[:], in0=self.pS[:, 32:64], in1=cs[:], op=ALU.subtract), reads=["pS", "cs"], writes=["dd"])
                P.op("act", lambda e: e.activation(out=dd[:], in_=dd[:], func=AF.Exp), reads=["dd"], writes=["dd"])
                P.op("dve", lambda e: e.tensor_tensor(out=wdt[:], in0=dd[:], in1=dtc[:], op=ALU.mult), reads=["dd", "dtc"], writes=["wdt"])
                P.op("pe", lambda e: e.transpose(out=self.pS[0:32, 128:256], in_=cs[:], identity=self.ident_f[:]), reads=["cs", "ident_f"], writes=["pS"])
                P.op("act", lambda e: e.copy(out=csT[:], in_=self.pS[0:32, 128:256]), reads=["pS"], writes=["csT"])
                P.op("dve", lambda e: e.tensor_scalar(out=ncsT[:], in0=self.pS[0:32, 128:256], scalar1=-1.0, scalar2=None, op0=ALU.mult), reads=["pS"], writes=["ncsT"])
                xv = xtok[:].rearrange("p c (r d) -> p (c r) d", d=64)
                P.op("dve", lambda e: e.tensor_tensor(out=xdt[:], in0=xv, in1=dtc[:].unsqueeze(2).to_broadcast([128, 32, 64]), op=ALU.mult), reads=["xtok", "dtc"], writes=["xdt"])
                P.op("pool", lambda e: e.tensor_tensor(out=xdd[:], in0=xv, in1=wdt[:].unsqueeze(2).to_broadcast([128, 32, 64]), op=ALU.mult), reads=["xtok", "wdt"], writes=["xdd"])
                seg = self.pA[:, 0, :]
                pY = self.pA[:, 1, 0:256]
                pO = self.pA[:, 1, 256:512]
                pH = self.pB[:, 0, 0:256]
                pcb = self.pB[:, 0, 256:384]
                for g in range(8):
                    P.op("pe", lambda e, g=g: e.matmul(pcb, BT[:, g, l0:l0 + 128], CT[:, g, l0:l0 + 128], start=True, stop=True), reads=["BT", "CT"], writes=["pB"])
                    P.op("act", lambda e: e.copy(out=cbs[:], in_=pcb), reads=["pB"], writes=["cbs"])
                    P.op("pe", lambda e, g=g: e.matmul(seg, ncsT[:], self.ident_f[0:32, 4 * g:4 * g + 4].unsqueeze(2).to_broadcast([32, 4, 128]), start=True, stop=False), reads=["ncsT", "ident_f"], writes=["pA"])
                    P.op("pe", lambda e: e.matmul(seg, self.negI[:], mk[:].rearrange("k r s -> k (r s)"), start=False, stop=False), reads=["negI", mkk], writes=["pA"])
                    for r in range(4):
                        P.op("pe", lambda e, g=g, r=r: e.matmul(seg[:, r * 128:(r + 1) * 128], self.ident_f[0:32, 4 * g + r:4 * g + r + 1].to_broadcast([32, 128]), csT[:], start=False, stop=(r == 3)), reads=["csT", "ident_f"], writes=["pA"])
                    P.op("act", lambda e: e.activation(out=Lt[:].rearrange("p r s -> p (r s)"), in_=seg, func=AF.Exp), reads=["pA"], writes=["Lt"])
                    P.op("dve", lambda e: e.tensor_tensor(out=Mt[:], in0=Lt[:], in1=cbs[:].unsqueeze(1).to_broadcast([128, 4, 128]), op=ALU.mult), reads=["Lt", "cbs"], writes=["Mt"])
                    for r in range(4):
                        P.op("pe", lambda e, g=g, r=r: e.matmul(pY[:, r * 64:(r + 1) * 64], Mt[:, r, :], xdt[:, 4 * g + r, :], start=True, stop=True), reads=["Mt", "xdt"], writes=["pA"])
                    P.op("pe", lambda e, g=g: e.matmul(pO, CT[:, g, l0:l0 + 128], Hb[:, g, :], start=True, stop=True), reads=["CT", "Hb"], writes=["pA"])
                    P.op("dve", lambda e, g=g: e.tensor_tensor(out=tmpo[:], in0=pO.rearrange("p (r d) -> p r d", d=64), in1=Ee[:, 4 * g:4 * g + 4].unsqueeze(2).to_broadcast([128, 4, 64]), op=ALU.mult), reads=["pA", "Ee"], writes=["tmpo"])
                    P.op("dve", lambda e, g=g: e.tensor_tensor(out=ych[:, g * 256:(g + 1) * 256], in0=pY, in1=tmpo[:].rearrange("p r d -> p (r d)"), op=ALU.add), reads=["pA", "tmpo"], writes=["ych"])
                    P.op("pe", lambda e, g=g: e.matmul(pH, btok[:, g, :], xdd[:, 4 * g:4 * g + 4, :].rearrange("p r d -> p (r d)"), start=True, stop=True), reads=["btok", "xdd"], writes=["pB"])
                    P.op("pool", lambda e, g=g: e.tensor_tensor(out=Hf[:, g, :].rearrange("p (r d) -> p r d", d=64), in0=Hf[:, g, :].rearrange("p (r d) -> p r d", d=64), in1=dtot[:, 4 * g:4 * g + 4].unsqueeze(2).to_broadcast([128, 4, 64]), op=ALU.mult), reads=["Hf", "dtot", "Hb"], writes=["Hf"])
                    P.op("dve", lambda e, g=g: e.tensor_tensor(out=Hf[:, g, :], in0=pH, in1=Hf[:, g, :], op=ALU.add), reads=["pB", "Hf"], writes=["Hf"])
                    P.op("act", lambda e, g=g: e.copy(out=Hb[:, g, :], in_=Hf[:, g, :]), reads=["Hf"], writes=["Hb"])

            for bwd in (False, True):
                P.op("pool", lambda e: e.memset(Hf[:].rearrange("p g d -> p (g d)"), 0.0), reads=["Hb"], writes=["Hf"])
                P.op("dve", lambda e: e.memset(Hb[:].rearrange("p g d -> p (g d)"), 0.0), writes=["Hb"])
                tts = range(NQ - 1, -1, -1) if bwd else range(NQ)
                for tt in tts:
                    prep_tile(tt)
                    subs = range(3, -1, -1) if bwd else range(4)
                    for sub in subs:
                        t = tt * 4 + sub
                        chunk(t, sub * 128, bwd)
                        if not bwd:
                            P.dma("sp", self.yf_d[t * 128:(t + 1) * 128, :], ych[:], reads=["ych"], writes=["yf_d"])
                            continue
                        P.dma("sp", yft[:], self.yf_d[t * 128:(t + 1) * 128, :], reads=["yf_d"], writes=["yft"])
                        P.op("dve", lambda e: e.tensor_tensor(out=ych[:], in0=ych[:], in1=yft[:], op=ALU.add), reads=["ych", "yft"], writes=["ych"])
                        xv = xtok[:].rearrange("p c (r d) -> p (c r) d", d=64)
                        P.op("pool", lambda e, xv=xv: e.tensor_tensor(out=yft[:].rearrange("p (r d) -> p r d", d=64), in0=xv, in1=Dbc[:].unsqueeze(2).to_broadcast([128, 32, 64]), op=ALU.mult), reads=["xtok", "Dbc", "ych"], writes=["yft"])
                        P.op("dve", lambda e: e.tensor_tensor(out=ych[:], in0=ych[:], in1=yft[:], op=ALU.add), reads=["ych", "yft"], writes=["ych"])
                        P.dma("sp", yft[:], self.z_d[t * 128:(t + 1) * 128, :], reads=["z_d", "ych"], writes=["yft"])
                        P.op("dve", lambda e: e.tensor_tensor(out=ych[:], in0=ych[:], in1=yft[:], op=ALU.mult), reads=["ych", "yft"], writes=["ych"])
                        P.op("pool", lambda e: e.memset(self.ss[:, 2:3], 0.0), writes=["ss3"])
                        P.op("dve", lambda e: e.scalar_tensor_tensor(out=yft[:], in0=ych[:], scalar=1.0, in1=ych[:], op0=ALU.mult, op1=ALU.mult, accum_out=self.ss[:, 2:3]), reads=["ych", "ss3"], writes=["yft", "ss3"])
                        self.rstd_from_ss(self.ss[:, 2:3], "ss3", self.rs[:, 2:3], "rs3", 2048)
                        P.op("dve", lambda e: e.tensor_scalar(out=ynb[:], in0=ych[:], scalar1=self.rs[:, 2:3], scalar2=None, op0=ALU.mult), reads=["ych", "rs3"], writes=["ynb"])
                        self.transpose_into(ynb, "ynb", 16, ynT, "ynT", 0)
                        for half in range(2):
                            for c in range(16):
                                P.op("pe", lambda e, c=c, half=half: e.matmul(self.pA[:, half, :], ynT[:, c, :], Wout[:, c, half * 512:(half + 1) * 512], start=(c == 0), stop=(c == 15)), reads=["ynT", "Wout"], writes=["pA"])
                        self.tail(li, t, self.pA, "pA", xin, xout)
            P.barrier()

    def build(self):
        L, NT = self.L, self.NT
        nc = bass.Bass("TRN2", target_bir_lowering=False)
        self.nc = nc
        NL = len(self.layers)
        nssd = max(1, sum(1 for t in self.layers if t == "ssd"))
        natt = max(1, sum(1 for t in self.layers if t == "att"))
        x_d = self.dram_in("xs", [L, D])
        self.p_d = self.dram_in("ps", [NL, L, PLE])
        tm_d = self.dram_in("tmask", [128, NT])
        self.kext = self.dram_in("kext", [8, 5, L], BF16)
        self.qext = self.dram_in("qext", [3, 8, 5, L], BF16)
        self.dtab = self.dram_in("dtab", [8, 4, 128, 512])
        cst = self.dram_in("cst_f", [6, 128, 128])
        cstb = self.dram_in("cst_b", [4, 128, 128], BF16)
        selB_d = self.dram_in("selB_d", [32, 32, 128])
        shapes = dict(
            pre_norm_g=[NL, D], ssd_w_in=[nssd, D, 6208], ssd_conv_w=[nssd, 5, 4096], ssd_conv_b=[nssd, 4096],
            ssd_dt_bias=[nssd, 64], ssd_a_log=[nssd, 64], ssd_d_skip=[nssd, 32], ssd_norm_g=[nssd, 2048],
            ssd_w_out=[nssd, 2048, D], att_w_in=[natt, D, 4096], att_q_norm_g=[natt, 64], att_k_norm_g=[natt, 64],
            att_lam_q1=[natt, 64], att_lam_k1=[natt, 64], att_lam_q2=[natt, 64], att_lam_k2=[natt, 64],
            att_sub_norm_g=[natt, 128], att_w_out=[natt, D, D], ple_w_proj=[NL, PLE, D], ple_norm_g=[NL, D],
            ple_gate_norm_g=[NL, D], ple_w_gate=[NL, D, D])
        self.w = {k: self.dram_in(k, s) for k, s in shapes.items()}
        y_d = nc.dram_tensor("y", [L, D], F32, kind="ExternalOutput").ap()
        xres = self.scratch("xres", [L, D], F32)
        self.qT_d = self.scratch("qT_d", [8, 128, L], BF16)
        self.kT_d = self.scratch("kT_d", [8, 128, L], BF16)
        self.v_d = self.scratch("v_d", [L, D], BF16)
        self.sg_d = self.scratch("sg_d", [D, L], BF16)
        self.og_d = self.scratch("og_d", [D, L], BF16)
        if "ssd" in self.layers:
            self.xbc_d = self.scratch("xbc_d", [4096, L], F32)
            self.z_d = self.scratch("z_d", [L, 2048], F32)
            self.dt_d = self.scratch("dt_d", [L, 64], F32)
            self.yf_d = self.scratch("yf_d", [L, 2048], F32)
        with ExitStack() as es:
            block = es.enter_context(nc.Block())
            P = Prog(nc, es, block)
            self.P = P
            g = lambda n, s, d: self.sb(es, n, s, d)
            ps = lambda n, s, d: es.enter_context(nc.psum_tensor(n, s, d))
            self.pAB = ps("pAB", [128, 4, 512], F32)
            self.pA = self.pAB[:, 0:2, :]
            self.pB = self.pAB[:, 2:4, :]
            self.pC = ps("pC", [128, 2, 512], F32)
            self.pT = ps("pT", [128, 8, 128], BF16)
            self.pS = ps("pS", [128, 512], F32)
            self.wst = [g(f"wst{i}", [128, 1024], F32) for i in range(2)]
            self.wcnt = 0
            self.ident = g("ident", [128, 128], BF16)
            self.negI = g("negI", [128, 128], BF16)
            self.maskF = g("maskF", [128, 4, 128], BF16)
            self.maskB = g("maskB", [128, 4, 128], BF16)
            self.ident_f = g("ident_f", [128, 128], F32)
            self.blk1 = g("blk1", [128, 128], F32)
            self.ones_f = g("ones_f", [128, 128], F32)
            self.U_f = g("U_f", [128, 128], F32)
            self.Lo_f = g("Lo_f", [128, 128], F32)
            self.epsc = g("epsc", [128, 1], F32)
            self.tm = g("tm", [128, NT], F32)
            self.gcols = g("gcols", [128, 8, 8], F32)
            self.gple = g("gple", [128, D], F32)
            self.xt = g("xt", [128, D], F32)
            self.xm = g("xm", [128, D], F32)
            self.junk = g("junk", [128, 1024], F32)
            self.gs = g("gs", [128, D], F32)
            self.ev = g("ev", [128, D], F32)
            self.hb = g("hb", [128, D], BF16)
            self.h2T = g("h2T", [128, 8, 128], BF16)
            self.pt = g("pt", [128, PLE], F32)
            self.pb = g("pb", [128, PLE], BF16)
            self.ppT = g("ppT", [128, 2, 128], BF16)
            self.ss = g("ss", [128, 4], F32)
            self.rs = g("rs", [128, 4], F32)
            P.op("pool", lambda e: e.memset(self.epsc[:], EPS), writes=["epsc"])
            P.dma("sp", self.ident[:], cstb[0], writes=["ident"])
            P.dma("sp", self.negI[:], cstb[1], writes=["negI"])
            for r in range(4):
                P.dma("sp", self.maskF[:, r, :], cstb[2], writes=["maskF"])
                P.dma("sp", self.maskB[:, r, :], cstb[3], writes=["maskB"])
            for i, tl in enumerate([self.ident_f, self.blk1, self.ones_f, self.U_f, self.Lo_f]):
                P.dma("pool", tl[:], cst[i], writes=[["ident_f", "blk1", "ones_f", "U_f", "Lo_f"][i]])
            P.dma("pool", self.tm[:], tm_d, writes=["tm"])
            for li in range(NL):
                P.dma("sp", self.gcols[:, li, :], self.w["pre_norm_g"][li].rearrange("(c p) -> p c", p=128), writes=["gcols"], allow_slow_non_contiguous=True)
                P.dma("sp", self.gcols[:, 4 + li, :], self.w["ple_gate_norm_g"][li].rearrange("(c p) -> p c", p=128), writes=["gcols"], allow_slow_non_contiguous=True)
            js = {"ssd": 0, "att": 0}
            for li, typ in enumerate(self.layers):
                xin = x_d if li == 0 else xres
                xout = y_d if li == NL - 1 else xres
                j = js[typ]
                js[typ] += 1
                if typ == "att":
                    self.att_layer(li, j, xin, xout)
                else:
                    self.ssd_layer(li, j, xin, xout)
            P.finish()
            self.ninstr = P.ninstr
        return nc


def host_consts(L, lens):
    bf = ml_dtypes.bfloat16
    slopes = np.array([2.0 ** (-8.0 * (i + 1) / 8) for i in range(8)], dtype=np.float64)
    pos = np.arange(L)
    hi = (pos // 128).astype(np.float64) * 128
    lo = (pos % 128).astype(np.float64)
    kext = np.zeros((8, 5, L), np.float64)
    qext = np.zeros((3, 8, 5, L), np.float64)
    for h in range(8):
        s = slopes[h]
        kext[h, 0] = s * hi
        kext[h, 1] = s * lo
        kext[h, 2] = 1
        kext[h, 3] = 1
        kext[h, 4] = np.where(pos < lens, 0.0, NEG)
        qext[0, h] = np.stack([np.ones(L), np.ones(L), -s * hi, -s * lo, np.ones(L)])
        qext[1, h] = np.stack([-np.ones(L), -np.ones(L), s * hi, s * lo, np.ones(L)])
        qext[2, h] = np.stack([np.zeros(L), np.zeros(L), np.zeros(L), np.zeros(L), np.ones(L)])
    kr = np.arange(128)[:, None]
    qr = np.arange(512)[None, :]
    dtab = np.zeros((8, 4, 128, 512), np.float32)
    for h in range(8):
        for jj in range(4):
            dtab[h, jj] = -slopes[h] * np.abs(qr - kr - 128 * jj)
    i128 = np.eye(128)
    blk1 = np.kron(np.eye(2), np.ones((64, 64)))
    k_ = np.arange(128)[:, None]
    l_ = np.arange(128)[None, :]
    U = (k_ <= l_).astype(np.float64)
    Lo = (k_ >= l_).astype(np.float64)
    cst_f = np.stack([i128, blk1, np.ones((128, 128)), U, Lo, np.zeros((128, 128))]).astype(np.float32)
    cst_b = np.stack([i128, NEG * i128, (l_ < k_).astype(np.float64), (l_ > k_).astype(np.float64)]).astype(bf)
    selB = np.zeros((32, 32, 128), np.float32)
    for r in range(32):
        selB[r, r, :] = 1
    tmask = np.ascontiguousarray((pos < lens).astype(np.float32).reshape(L // 128, 128).T)
    return dict(kext=kext.astype(bf), qext=qext.astype(bf), dtab=dtab, cst_f=cst_f, cst_b=cst_b, selB_d=selB, tmask=tmask)


_CACHE = {}


def run_trunk(xs_list, ps_list, lens_list, weights, layers, L):
    NL = len(layers)
    lam_inits = [0.8 - 0.6 * math.exp(-0.3 * i) for i in range(NL)]
    key = (L, tuple(layers))
    if key not in _CACHE:
        kb = KB(L, layers, lam_inits)
        _CACHE[key] = (kb, kb.build())
    kb, nc = _CACHE[key]
    wmap = {}
    for k, v in weights.items():
        a = np.ascontiguousarray(np.asarray(v, dtype=np.float32))
        if k in ("ssd_dt_bias", "ssd_a_log"):
            a = a.reshape(a.shape[0], 64)
        wmap[k] = a
    in_maps = []
    for c in range(8):
        m = dict(wmap)
        m.update(host_consts(L, lens_list[c]))
        m["xs"] = xs_list[c]
        m["ps"] = ps_list[c]
        in_maps.append(m)
    res = run_bass_kernel_spmd(nc, in_maps, core_ids=list(range(8)))
    return [r["y"] for r in res.results]


LAYERS = ["ssd", "att", "ssd", "att"]
WNAMES = ["pre_norm_g", "ssd_w_in", "ssd_conv_w", "ssd_conv_b", "ssd_dt_bias", "ssd_a_log", "ssd_d_skip", "ssd_norm_g",
          "ssd_w_out", "att_w_in", "att_q_norm_g", "att_k_norm_g", "att_lam_q1", "att_lam_k1", "att_lam_q2", "att_lam_k2",
          "att_sub_norm_g", "att_w_out", "ple_w_proj", "ple_norm_g", "ple_gate_norm_g", "ple_w_gate"]


def kernel(x_prompt, x_sample, p_prompt, p_sample, **weights):
    L = 16384
    xp = np.asarray(x_prompt, dtype=np.float32)
    xsm = np.asarray(x_sample, dtype=np.float32)
    pp = np.asarray(p_prompt, dtype=np.float32)
    psm = np.asarray(p_sample, dtype=np.float32)
    B, S = xp.shape[0], xp.shape[1]
    NL = pp.shape[0]
    xs_list, ps_list, lens = [], [], []
    for c in range(8):
        xs = np.zeros((L, D), np.float32)
        ps = np.zeros((NL, L, PLE), np.float32)
        if c < B:
            xs[:S] = xp[c]
            ps[:, :S] = pp[:, c]
            lens.append(S)
        elif c == B:
            xs[:] = xsm[0]
            ps[:] = psm[:, 0]
            lens.append(L)
        else:
            lens.append(L)
        xs_list.append(xs)
        ps_list.append(ps)
    ys = run_trunk(xs_list, ps_list, lens, weights, LAYERS, L)
    y_prompt = np.stack([ys[c][:S] for c in range(B)]).astype(np.float32)
    y_sample = ys[B][None].astype(np.float32)
    return (y_prompt, y_sample)
```
